# Optimizing a Trainium2 kernel written in Bass

```python
import math
import jax, jax.numpy as jnp
from jax import lax
import numpy as np

D_MODEL = 2048
BATCH = 4
SEQ = 2048
DEPTH = 4

GRID_W = 64
CTX_LEN = 256
EPS = 1e-6
F32 = jnp.float32

D_MIX = D_MODEL
D_CONV = D_MIX // 4
D_SSM = D_MIX // 4
D_ATTN = D_MIX - D_CONV - D_SSM
CONV_W = 3

MLA_HEADS = 8
QK_NOPE = 128
QK_ROPE = 64
V_HEAD = D_ATTN // MLA_HEADS
Q_RANK = 512
KV_RANK = 256
ROPE_BASE = 10000.0
MLA_SCALE = (QK_NOPE + QK_ROPE) ** -0.5
Q_BLOCK = 128

SSM_GROUP = 16
SSM_GROUPS = D_SSM // SSM_GROUP
SSM_STATE = 64
DT_MIN = 0.001
DT_MAX = 0.1

D_FF = -(-8 * D_MODEL // (3 * 256)) * 256

PROJ_SIZES = (D_CONV, D_CONV, D_CONV, Q_RANK, KV_RANK, QK_ROPE, D_SSM)
D_PROJ = sum(PROJ_SIZES)
PROJ_SPLITS = tuple(int(s) for s in np.cumsum(PROJ_SIZES)[:-1])

kernel_name = 'hybrid_conv_mla_s5_dit_block'


def rms_norm(x, g):
    xf = x.astype(F32)
    y = xf * lax.rsqrt(jnp.mean(xf * xf, axis=-1, keepdims=True) + EPS)
    return (y * g.astype(F32)).astype(x.dtype)


def modulate(h, shift, scale):
    return h * (1.0 + scale) + shift


def swiglu(h, w_gate, w_up, w_down):
    return (jax.nn.silu(h @ w_gate) * (h @ w_up)) @ w_down


def axial_rope_tables(rows):
    row = jnp.repeat(jnp.arange(rows, dtype=F32), GRID_W)
    col = jnp.tile(jnp.arange(GRID_W, dtype=F32), rows)
    n_freq = QK_ROPE // 4
    inv = ROPE_BASE ** (-jnp.arange(n_freq, dtype=F32) / n_freq)
    ang = jnp.stack([row[:, None] * inv, col[:, None] * inv], axis=1)
    return jnp.cos(ang), jnp.sin(ang)


def apply_axial_rope(x, cos, sin):
    xs = x.astype(F32).reshape(*x.shape[:-1], 2, 2, QK_ROPE // 4)
    x1, x2 = xs[..., 0, :], xs[..., 1, :]
    out = jnp.stack([x1 * cos - x2 * sin, x2 * cos + x1 * sin], axis=-2)
    return out.reshape(x.shape).astype(x.dtype)


def depthwise_conv(x, w):
    pad = CONV_W // 2
    return lax.conv_general_dilated(x, w[:, None, :], window_strides=(1,), padding=((pad, pad),),
                                    dimension_numbers=('NWC', 'WIO', 'NWC'),
                                    feature_group_count=x.shape[-1])


def short_conv_mixer(h, b_gate, c_gate, conv_w):
    return b_gate * depthwise_conv(c_gate * h, conv_w)


def mla_queries(c_q, q_norm, w_uq, rope):
    bn, L, _ = c_q.shape
    q = (rms_norm(c_q, q_norm) @ w_uq).reshape(bn, L, MLA_HEADS, QK_NOPE + QK_ROPE)
    q_nope, q_rope = q[..., :QK_NOPE], q[..., QK_NOPE:]
    if rope is not None:
        cos, sin = rope
        q_rope = apply_axial_rope(q_rope, cos[:, None], sin[:, None])
    return q_nope, q_rope


def mla_keys_values(c_kv, k_rope, kv_norm, w_ukv, rope):
    bn, L, _ = c_kv.shape
    kv = (rms_norm(c_kv, kv_norm) @ w_ukv).reshape(bn, L, MLA_HEADS, QK_NOPE + V_HEAD)
    if rope is not None:
        cos, sin = rope
        k_rope = apply_axial_rope(k_rope, cos, sin)
    return kv[..., :QK_NOPE], k_rope, kv[..., QK_NOPE:]


def mla_softmax_attend(qn, qr, kn, kr, v):
    s = jnp.einsum('bqhd,bkhd->bhqk', qn, kn) + jnp.einsum('bqhr,bkr->bhqk', qr, kr)
    p = jax.nn.softmax(s.astype(F32) * MLA_SCALE, axis=-1).astype(v.dtype)
    return jnp.einsum('bhqk,bkhd->bqhd', p, v)


def latent_attention(qn, qr, kn, kr, v):
    bn, L = qn.shape[:2]
    nb = L // Q_BLOCK

    def blocks(t):
        return jnp.moveaxis(t.reshape(bn, nb, Q_BLOCK, *t.shape[2:]), 1, 0)

    out = lax.map(lambda q: mla_softmax_attend(q[0], q[1], kn, kr, v), (blocks(qn), blocks(qr)))
    return jnp.moveaxis(out, 0, 1).reshape(bn, L, MLA_HEADS * V_HEAD)


def s5_discretise(a_re, a_im, log_dt, b_re, b_im):
    a = lax.complex(a_re.astype(F32), a_im.astype(F32))
    dt = jnp.exp(log_dt.astype(F32))[:, None]
    a_bar = jnp.exp(a * dt)
    b = lax.complex(b_re.astype(F32), b_im.astype(F32))
    b_bar = ((a_bar - 1.0) / a)[..., None] * b
    return a_bar, b_bar


def _linear_recurrence(e1, e2):
    a1, b1 = e1
    a2, b2 = e2
    return a1 * a2, a2 * b1 + b2


def s5_scan(u, a_bar, b_bar, h0, reverse):
    bu = jnp.einsum('blgc,gpc->blgp', u.astype(jnp.complex64), b_bar)
    if h0 is not None:
        first = -1 if reverse else 0
        bu = bu.at[:, first].add(a_bar * h0)
    a = jnp.broadcast_to(a_bar, bu.shape)
    _, h = lax.associative_scan(_linear_recurrence, (a, bu), axis=1, reverse=reverse)
    return h


def s5_readout(h, c_re, c_im):
    c = lax.complex(c_re.astype(F32), c_im.astype(F32))
    return jnp.einsum('blgp,gcp->blgc', h, c).real


def s5_output(u, h_f, h_b, c_re, c_im, ssm_d, w_glu, b_glu, dtype):
    bn, L = u.shape[:2]
    y = (s5_readout(h_f, c_re[0], c_im[0]) + s5_readout(h_b, c_re[1], c_im[1])).reshape(bn, L, D_SSM)
    y = y + ssm_d.astype(F32) * u.reshape(bn, L, D_SSM)
    g = jax.nn.gelu(y)
    return (g * jax.nn.sigmoid(g @ w_glu.astype(F32) + b_glu.astype(F32))).astype(dtype)


def merge_head_groups(y_conv, y_att, y_ssm, mix_norm, w_o):
    y = jnp.concatenate([
        rms_norm(y_conv, mix_norm[:D_CONV]),
        rms_norm(y_att, mix_norm[D_CONV:D_CONV + D_ATTN]),
        rms_norm(y_ssm, mix_norm[D_CONV + D_ATTN:]),
    ], axis=-1)
    return y @ w_o


def hybrid_mixer(hl, hc, rope, ctx_out, w_in, conv_w, q_norm, w_uq, kv_norm, w_ukv,
                 a_re, a_im, log_dt, b_re, b_im, c_re, c_im, ssm_d, w_glu, b_glu, mix_norm, w_o):
    bn, L, _ = hl.shape
    lc = hc.shape[1]
    hconv_l, bg_l, cg_l, cq_l, ckv_l, kr_l, u_l = jnp.split(hl @ w_in, PROJ_SPLITS, axis=-1)
    hconv_c, bg_c, cg_c, cq_c, ckv_c, kr_c, u_c = jnp.split(hc @ w_in, PROJ_SPLITS, axis=-1)

    conv_l = short_conv_mixer(hconv_l, bg_l, cg_l, conv_w)

    kn_c, krc, v_c = mla_keys_values(ckv_c, kr_c, kv_norm, w_ukv, None)
    kn_l, krl, v_l = mla_keys_values(ckv_l, kr_l, kv_norm, w_ukv, rope)
    qn_l, qr_l = mla_queries(cq_l, q_norm, w_uq, rope)
    att_l = latent_attention(qn_l, qr_l,
                             jnp.concatenate([kn_c, kn_l], axis=1),
                             jnp.concatenate([krc, krl], axis=1),
                             jnp.concatenate([v_c, v_l], axis=1))

    uc = u_c.astype(F32).reshape(bn, lc, SSM_GROUPS, SSM_GROUP)
    ul = u_l.astype(F32).reshape(bn, L, SSM_GROUPS, SSM_GROUP)
    a_f, bb_f = s5_discretise(a_re[0], a_im[0], log_dt[0], b_re[0], b_im[0])
    a_b, bb_b = s5_discretise(a_re[1], a_im[1], log_dt[1], b_re[1], b_im[1])
    hc_f = s5_scan(uc, a_f, bb_f, None, False)
    hc_b = s5_scan(uc, a_b, bb_b, None, True)
    hl_f = s5_scan(ul, a_f, bb_f, hc_f[:, -1], False)
    hl_b = s5_scan(ul, a_b, bb_b, hc_b[:, 0], True)
    ssm_l = s5_output(ul, hl_f, hl_b, c_re, c_im, ssm_d, w_glu, b_glu, hl.dtype)

    yl = merge_head_groups(conv_l, att_l, ssm_l, mix_norm, w_o)
    if not ctx_out:
        return yl, None

    conv_c = short_conv_mixer(hconv_c, bg_c, cg_c, conv_w)
    qn_c, qr_c = mla_queries(cq_c, q_norm, w_uq, None)
    att_c = mla_softmax_attend(qn_c, qr_c, kn_c, krc, v_c).reshape(bn, lc, MLA_HEADS * V_HEAD)
    ssm_c = s5_output(uc, hc_f, hc_b, c_re, c_im, ssm_d, w_glu, b_glu, hc.dtype)
    yc = merge_head_groups(conv_c, att_c, ssm_c, mix_norm, w_o)
    return yl, yc


def setup_inputs(seed: int = 0) -> dict:
    key = jax.random.key(seed)
    ks = iter(jax.random.split(key, 40))

    def nrm(shape, scale):
        return jax.random.normal(next(ks), shape, F32) * scale

    G, P, Hg = SSM_GROUPS, SSM_STATE, SSM_GROUP
    n = jnp.arange(P, dtype=F32)
    return {
        'x': nrm((BATCH, SEQ, D_MODEL), 1.0),
        'c': nrm((BATCH, D_MODEL), 1.0),
        'ctx': nrm((BATCH, CTX_LEN, D_MODEL), 1.0),
        'c_ctx': nrm((D_MODEL,), 1.0),
        'w_ada': nrm((DEPTH, D_MODEL, 6 * D_MODEL), 0.5 * D_MODEL ** -0.5),
        'b_ada': nrm((DEPTH, 6 * D_MODEL), 0.02),
        'norm1_g': 1.0 + nrm((DEPTH, D_MODEL), 0.02),
        'norm2_g': 1.0 + nrm((DEPTH, D_MODEL), 0.02),
        'w_in': nrm((DEPTH, D_MODEL, D_PROJ), D_MODEL ** -0.5),
        'conv_w': nrm((DEPTH, CONV_W, D_CONV), CONV_W ** -0.5),
        'mla_q_norm': 1.0 + nrm((DEPTH, Q_RANK), 0.02),
        'w_uq': nrm((DEPTH, Q_RANK, MLA_HEADS * (QK_NOPE + QK_ROPE)), Q_RANK ** -0.5),
        'mla_kv_norm': 1.0 + nrm((DEPTH, KV_RANK), 0.02),
        'w_ukv': nrm((DEPTH, KV_RANK, MLA_HEADS * (QK_NOPE + V_HEAD)), KV_RANK ** -0.5),
        'ssm_a_re': -0.5 + nrm((DEPTH, 2, G, P), 0.01),
        'ssm_a_im': math.pi * n + nrm((DEPTH, 2, G, P), 0.01),
        'ssm_log_dt': jax.random.uniform(next(ks), (DEPTH, 2, G), F32, math.log(DT_MIN), math.log(DT_MAX)),
        'ssm_b_re': nrm((DEPTH, 2, G, P, Hg), (2 * Hg) ** -0.5),
        'ssm_b_im': nrm((DEPTH, 2, G, P, Hg), (2 * Hg) ** -0.5),
        'ssm_c_re': nrm((DEPTH, 2, G, Hg, P), (2 * P) ** -0.5),
        'ssm_c_im': nrm((DEPTH, 2, G, Hg, P), (2 * P) ** -0.5),
        'ssm_d': nrm((DEPTH, D_SSM), 1.0),
        'w_glu': nrm((DEPTH, D_SSM, D_SSM), D_SSM ** -0.5),
        'b_glu': nrm((DEPTH, D_SSM), 0.02),
        'mix_norm': 1.0 + nrm((DEPTH, D_MIX), 0.02),
        'w_o': nrm((DEPTH, D_MIX, D_MODEL), D_MIX ** -0.5),
        'w_gate': nrm((DEPTH, D_MODEL, D_FF), D_MODEL ** -0.5),
        'w_up': nrm((DEPTH, D_MODEL, D_FF), D_MODEL ** -0.5),
        'w_down': nrm((DEPTH, D_FF, D_MODEL), D_FF ** -0.5),
        'final_norm': 1.0 + nrm((D_MODEL,), 0.02),
    }


def reference(x, c, ctx, c_ctx, w_ada, b_ada, norm1_g, norm2_g, w_in, conv_w, mla_q_norm, w_uq,
              mla_kv_norm, w_ukv, ssm_a_re, ssm_a_im, ssm_log_dt, ssm_b_re, ssm_b_im, ssm_c_re,
              ssm_c_im, ssm_d, w_glu, b_glu, mix_norm, w_o, w_gate, w_up, w_down, final_norm):
    L = x.shape[1]
    ROWS = L // GRID_W
    rope = axial_rope_tables(ROWS)
    silu_c = jax.nn.silu(c)
    silu_cc = jax.nn.silu(c_ctx)
    xl, xc = x, ctx
    for i in range(DEPTH):
        ctx_out = i < DEPTH - 1
        sh1, sc1, g1, sh2, sc2, g2 = [m[:, None] for m in jnp.split(silu_c @ w_ada[i] + b_ada[i], 6, axis=-1)]
        csh1, csc1, cg1, csh2, csc2, cg2 = jnp.split(silu_cc @ w_ada[i] + b_ada[i], 6, axis=-1)

        hl = modulate(rms_norm(xl, norm1_g[i]), sh1, sc1)
        hc = modulate(rms_norm(xc, norm1_g[i]), csh1, csc1)
        yl, yc = hybrid_mixer(hl, hc, rope, ctx_out, w_in[i], conv_w[i], mla_q_norm[i], w_uq[i],
                              mla_kv_norm[i], w_ukv[i], ssm_a_re[i], ssm_a_im[i], ssm_log_dt[i],
                              ssm_b_re[i], ssm_b_im[i], ssm_c_re[i], ssm_c_im[i], ssm_d[i],
                              w_glu[i], b_glu[i], mix_norm[i], w_o[i])
        xl = xl + g1 * yl
        hl2 = modulate(rms_norm(xl, norm2_g[i]), sh2, sc2)
        xl = xl + g2 * swiglu(hl2, w_gate[i], w_up[i], w_down[i])

        if ctx_out:
            xc = xc + cg1 * yc
            hc2 = modulate(rms_norm(xc, norm2_g[i]), csh2, csc2)
            xc = xc + cg2 * swiglu(hc2, w_gate[i], w_up[i], w_down[i])
    return rms_norm(xl, final_norm)
```

```python
import math
from contextlib import ExitStack

import numpy as np
import concourse.bass as bass
import concourse.mybir as mybir
from concourse.bass_utils import run_bass_kernel_spmd

F32, BF16 = mybir.dt.float32, mybir.dt.bfloat16
AF = mybir.ActivationFunctionType
ALU = mybir.AluOpType

D = 2048
LAT = 2048
CTX = 256
NT = LAT + CTX
DFF = 5632
DPROJ = 2880
EPS = 1e-6
MLA_SCALE = 192.0 ** -0.5
TILES = [(0, 256, 1), (256, 512, 0), (768, 512, 0), (1280, 512, 0), (1792, 512, 0)]
TWO_PI = 2.0 * math.pi


class Buf:
    def __init__(self, name=""):
        self.name = name
        self.w = None
        self.r = {}
        self.ds = None


class DSem:
    def __init__(self, sem):
        self.sem = sem
        self.tot = 0


class Ker:
    def __init__(self, nc, es):
        self.nc = nc
        self.es = es
        self.eng = {"pe": nc.tensor, "dve": nc.vector, "act": nc.scalar, "pool": nc.gpsimd, "sp": nc.sync}
        self.sem = {k: es.enter_context(nc.semaphore("sem_" + k)) for k in self.eng}
        self.cnt = {k: 0 for k in self.eng}
        self.waited = {k: {} for k in self.eng}
        self.dpool = [DSem(es.enter_context(nc.semaphore("dsem%d" % i))) for i in range(48)]
        self.dfree = list(self.dpool)
        self.dbufs = []

    def _wait(self, e, dep):
        sem, val = dep
        w = self.waited[e]
        if w.get(sem.num, 0) >= val:
            return
        self.eng[e].wait_ge(sem, val)
        w[sem.num] = val

    def _deps(self, e, reads, writes):
        deps = []
        mysem = self.sem[e]
        for b in reads:
            if b.w:
                deps.append(b.w)
        for b in writes:
            if b.w:
                deps.append(b.w)
            for k, d in b.r.items():
                if d[0] is mysem:
                    continue
                deps.append(d)
        for d in deps:
            if e == "pe" and d[0] is mysem:
                continue
            self._wait(e, d)

    def op(self, e, fn, reads=(), writes=()):
        self._deps(e, reads, writes)
        ins = fn(self.eng[e])
        self.cnt[e] += 1
        ins.then_inc(self.sem[e], 1)
        me = (self.sem[e], self.cnt[e])
        for b in writes:
            b.w = me
            b.r = {}
        for b in reads:
            if b not in writes:
                b.r[e] = me

    def dma(self, q, out, in_, sb, reads=(), writes=(), grouped=False, **kw):
        if sb.ds is None:
            sb.ds = self.dfree.pop()
            self.dbufs.append(sb)
        ds = sb.ds
        self._deps(q, reads, writes)
        if ds.tot > 0 and not grouped:
            self._wait(q, (ds.sem, ds.tot))
        self.eng[q].dma_start(out=out, in_=in_, **kw).then_inc(ds.sem, 16)
        ds.tot += 16
        me = (ds.sem, ds.tot)
        for b in writes:
            b.w = me
            b.r = {}
        for b in reads:
            if b not in writes:
                b.r["d%d" % ds.sem.num] = me

    def barrier(self):
        alls = [(self.sem[k], self.cnt[k]) for k in self.eng] + [(d.sem, d.tot) for d in self.dpool]
        for e in self.eng:
            for d in alls:
                if d[1] > 0 and not (e == "pe" and d[0] is self.sem["pe"]):
                    self._wait(e, d)
        for b in self.dbufs:
            b.ds = None
        self.dbufs = []
        self.dfree = list(self.dpool)


def build(nl, dbg=False, last_real=True):
    nc = bass.Bass("TRN2", target_bir_lowering=False)
    es = ExitStack()
    K = Ker(nc, es)

    def din(name, shape, dt=F32):
        return nc.dram_tensor(name, list(shape), dt, kind="ExternalInput").ap()

    def dscr(name, shape, dt=F32):
        return nc.dram_tensor(name, list(shape), dt, kind="Internal").ap()

    xT = din("xT", [D, NT])
    cs_in = din("cs", [128, 32])
    w_ada = din("w_ada", [nl, D, 6 * D])
    bada2 = din("bada2", [nl, 2, 6 * D])
    n1g = din("n1g", [nl, 128, 16])
    n2g = din("n2g", [nl, 128, 16])
    mng = din("mng", [nl, 128, 16])
    qng = din("qng", [nl, 128, 4])
    kvng = din("kvng", [nl, 128, 2])
    ssd = din("ssd", [nl, 128, 4])
    bglu = din("bglu", [nl, 128, 4])
    convw = din("convw", [nl, 128, 12])
    fng = din("fng", [128, 16])
    w_in = din("w_in", [nl, D, DPROJ])
    w_krp = din("w_krp", [nl, D, 64])
    w_uq = din("w_uq", [nl, 512, 1536])
    w_uqp = din("w_uqp", [nl, 512, 512])
    w_ukv = din("w_ukv", [nl, 256, 2048])
    w_v = din("w_v", [nl, 256, 1024])
    ropec = din("ropec", [64, LAT])
    ropes = din("ropes", [64, LAT])
    ssa = din("ssa", [nl, 128, 2 * 3 * 16])
    bw_in = din("bw", [nl, 128, 2 * 2 * 16 * 128])
    cw_in = din("cw", [nl, 128, 2 * 2 * 16 * 128])
    w_glu = din("w_glu", [nl, 512, 512])
    w_o = din("w_o", [nl, D, D])
    w_gate = din("w_gate", [nl, D, DFF])
    w_up = din("w_up", [nl, D, DFF])
    w_down = din("w_down", [nl, DFF, D])
    iota_in = din("iota", [128, 256])
    i2_in = din("i2", [2, 2])
    outT = nc.dram_tensor("outT", [D, LAT], F32, kind="ExternalOutput").ap()

    XR = dscr("XR", [D, NT])
    CQ = dscr("CQ", [512, NT])
    CKV = dscr("CKV", [256, NT])
    KRs = dscr("KRs", [64, NT])
    KRPs = dscr("KRPs", [64, NT])
    Us = dscr("Us", [512, NT])
    MIX = dscr("MIX", [D, NT], BF16)
    QN = dscr("QN", [1024, NT], BF16)
    QR = dscr("QR", [512, NT], BF16)
    KN = dscr("KN", [1024, NT], BF16)
    KRR = dscr("KRR", [64, NT], BF16)
    Vs = dscr("Vs", [NT, 1024], BF16)
    ATT = dscr("ATT", [1024, NT])
    dbg_outs = {}

    B_XR = [Buf("XR%d" % i) for i in range(16)]
    B_scr = {n: Buf(n) for n in ["CQ", "CKV", "KRs", "KRPs", "Us", "MIX", "QN", "QR", "KN", "KRR", "Vs", "ATT"]}

    sbc = [0]

    def sb(name, shape, dt=F32, stack=None):
        sbc[0] += 1
        t = (stack or es).enter_context(nc.sbuf_tensor("%s_%d" % (name, sbc[0]), list(shape), dt))
        return t, Buf(name)

    ONES, bONES = sb("ONES", [128, 128], BF16)
    CS, bCS = sb("CS", [128, 32], BF16)
    CSF, bCSF = sb("CSF", [128, 32])
    I2, bI2 = sb("I2", [2, 2])
    MODC, bMODC = sb("MODC", [128, 192])
    GS1, bGS1 = sb("GS1", [128, 32])
    SH1, bSH1 = sb("SH1", [128, 32])
    GG1, bGG1 = sb("GG1", [128, 32])
    GS2, bGS2 = sb("GS2", [128, 32])
    SH2, bSH2 = sb("SH2", [128, 32])
    GG2, bGG2 = sb("GG2", [128, 32])
    psum = []
    for i in range(8):
        t = es.enter_context(nc.psum_tensor("ps%d" % i, [128, 512], F32))
        psum.append((t, Buf("ps%d" % i)))
    pcnt = {}

    def getps(lo=0, hi=8):
        c = pcnt.get((lo, hi), 0)
        pcnt[(lo, hi)] = c + 1
        return psum[lo + c % (hi - lo)]

    K.op("dve", lambda e: e.memset(ONES[:], 1.0), writes=[bONES])
    K.dma("sp", CSF[:], cs_in[:, :], bCSF, writes=[bCSF])
    K.dma("sp", I2[:], i2_in[:, :], bI2, writes=[bI2])
    K.op("act", lambda e: e.activation(out=CS[:], in_=CSF[:], func=AF.Silu), reads=[bCSF], writes=[bCS])
    with ExitStack() as ph:
        T0, bT0 = sb("T0", [128, NT], F32, ph)
        T1, bT1 = sb("T1", [128, NT], F32, ph)
        for kb in range(16):
            T, bT = (T0, bT0) if kb % 2 == 0 else (T1, bT1)
            K.dma("sp", T[:], xT[kb * 128:(kb + 1) * 128, :], bT, writes=[bT])
            K.dma("sp", XR[kb * 128:(kb + 1) * 128, :], T[:], bT, reads=[bT], writes=[B_XR[kb]])
        K.barrier()

    def rstd_from_psum(ps, bps, n, dim, RS, bRS):
        K.op("dve", lambda e: e.tensor_scalar(out=RS[:, :n], in0=ps[:, :n], scalar1=1.0 / dim, scalar2=EPS,
                                              op0=ALU.mult, op1=ALU.add), reads=[bps], writes=[bRS])
        K.op("act", lambda e: e.activation(out=RS[:, :n], in_=RS[:, :n], func=AF.Sqrt), reads=[bRS], writes=[bRS])
        K.op("dve", lambda e: e.reciprocal(out=RS[:, :n], in_=RS[:, :n]), reads=[bRS], writes=[bRS])

    def sumsq(blocks, n, SQ, bSQ):
        ps, bps = getps()
        nb = len(blocks)
        for i, (ap, bb) in enumerate(blocks):
            sq, bsq = SQ[i % 2], bSQ[i % 2]
            K.op("act", lambda e, ap=ap, sq=sq: e.activation(out=sq[:, :n], in_=ap, func=AF.Square),
                 reads=[bb], writes=[bsq])
            K.op("pe", lambda e, sq=sq, i=i: e.matmul(ps[:, :n], lhsT=ONES[:], rhs=sq[:, :n],
                                                       start=(i == 0), stop=(i == nb - 1)),
                 reads=[bsq, bONES], writes=[bps])
        return ps, bps

    def wview(wap, k):
        return wap.rearrange("(kb p) n -> p kb n", p=128)

    for li in range(nl):
        is_last = last_real and (li == nl - 1)
        tiles_out = TILES[1:] if is_last else TILES

        with ExitStack() as ph:
            MODROW, bMODROW = sb("MODROW", [2, 6 * D], F32, ph)
            BROW, bBROW = sb("BROW", [2, 6 * D], F32, ph)
            WA = [sb("WA%d" % i, [128, 16, 512], BF16, ph) for i in range(2)]
            NG1, bNG1 = sb("NG1", [128, 16], F32, ph)
            NG2, bNG2 = sb("NG2", [128, 16], F32, ph)
            K.dma("sp", BROW[:], bada2[li], bBROW, writes=[bBROW])
            K.dma("sp", NG1[:], n1g[li], bNG1, writes=[bNG1])
            K.dma("sp", NG2[:], n2g[li], bNG2, writes=[bNG2])
            wav = wview(w_ada[li], 16)
            for ch in range(24):
                W, bW = WA[ch % 2]
                K.dma("pool", W[:], wav[:, :, ch * 512:(ch + 1) * 512], bW, writes=[bW])
                ps, bps = getps()
                for kb in range(16):
                    K.op("pe", lambda e, kb=kb, W=W: e.matmul(ps[0:2, :], lhsT=CS[:, kb * 2:kb * 2 + 2], rhs=W[:, kb, :],
                                                              start=(kb == 0), stop=(kb == 15)),
                         reads=[bW, bCS], writes=[bps])
                K.op("dve", lambda e, ch=ch: e.tensor_tensor(out=MODROW[:, ch * 512:(ch + 1) * 512], in0=ps[0:2, :],
                                                             in1=BROW[:, ch * 512:(ch + 1) * 512], op=ALU.add),
                     reads=[bps, bBROW], writes=[bMODROW])
            ps, bps = getps()
            for blk in range(96):
                K.op("pe", lambda e, blk=blk: e.matmul(ps[:, blk * 2:blk * 2 + 2], lhsT=MODROW[0:2, blk * 128:(blk + 1) * 128],
                                                       rhs=I2[0:2, 0:2], start=True, stop=True),
                     reads=[bMODROW, bI2], writes=[bps])
            K.op("dve", lambda e: e.tensor_copy(out=MODC[:], in_=ps[:, 0:192]), reads=[bps], writes=[bMODC])
            mv = MODC[:].rearrange("p (j kb r) -> p j kb r", j=6, kb=16, r=2)
            for r in range(2):
                sl = slice(r * 16, (r + 1) * 16)
                for (dst, bdst, jsc, jsh, jg, SH, bSH, GG, bGG, NG, bNG) in (
                        (GS1, bGS1, 1, 0, 2, SH1, bSH1, GG1, bGG1, NG1, bNG1),
                        (GS2, bGS2, 4, 3, 5, SH2, bSH2, GG2, bGG2, NG2, bNG2)):
                    K.op("dve", lambda e, dst=dst, jsc=jsc, NG=NG: e.scalar_tensor_tensor(
                        out=dst[:, sl], in0=mv[:, jsc, :, r], scalar=1.0, in1=NG[:], op0=ALU.add, op1=ALU.mult),
                        reads=[bMODC, bNG], writes=[bdst])
                    K.op("dve", lambda e, SH=SH, jsh=jsh: e.tensor_copy(out=SH[:, sl], in_=mv[:, jsh, :, r]),
                         reads=[bMODC], writes=[bSH])
                    K.op("dve", lambda e, GG=GG, jg=jg: e.tensor_copy(out=GG[:, sl], in_=mv[:, jg, :, r]),
                         reads=[bMODC], writes=[bGG])
            K.barrier()

        def norm_mod(tiles, HT, bHT, GS, bGS, SH, bSH, ph, dst_off=0):
            XT, bXT = sb("XTn", [128, 16, 512], F32, ph)
            bXTk = [Buf("XTk%d" % k) for k in range(16)]
            SQ = [None, None]
            bSQ = [None, None]
            for i in range(2):
                SQ[i], bSQ[i] = sb("SQn%d" % i, [128, 512], BF16, ph)
            RS, bRS = sb("RSn", [128, 512], F32, ph)
            TM = [sb("TMn%d" % i, [128, 512], F32, ph) for i in range(2)]
            for (c0, n, r) in tiles:
                for kb in range(16):
                    K.dma("sp", XT[:, kb, :n], XR[kb * 128:(kb + 1) * 128, c0:c0 + n], bXTk[kb],
                          reads=[B_XR[kb]], writes=[bXTk[kb]])
                ps, bps = sumsq([(XT[:, kb, :n], bXTk[kb]) for kb in range(16)], n, SQ, bSQ)
                rstd_from_psum(ps, bps, n, D, RS, bRS)
                for kb in range(16):
                    tm, btm = TM[kb % 2]
                    K.op("dve", lambda e, kb=kb, tm=tm: e.tensor_tensor(out=tm[:, :n], in0=XT[:, kb, :n], in1=RS[:, :n],
                                                                       op=ALU.mult),
                         reads=[bXTk[kb], bRS], writes=[btm])
                    K.op("act", lambda e, kb=kb, tm=tm: e.activation(
                        out=HT[:, kb, c0 - dst_off:c0 - dst_off + n], in_=tm[:, :n], func=AF.Identity,
                        bias=SH[:, r * 16 + kb:r * 16 + kb + 1], scale=GS[:, r * 16 + kb:r * 16 + kb + 1]),
                        reads=[btm, bSH, bGS], writes=[bHT])

        def proj(groups, kblocks, rhs_fn, rhs_bufs, tiles, consumer, wpool):
            for gi, segs in enumerate(groups):
                wts = []
                for si, (wv, c0, M) in enumerate(segs):
                    W, bW = wpool[(gi % 2) * 3 + si]
                    K.dma("pool", W[:, :kblocks, :M], wv[:, :, c0:c0 + M], bW, writes=[bW])
                    wts.append((W, bW, M))
                for tile in tiles:
                    (t0, n, r) = tile
                    outs = []
                    for (W, bW, M) in wts:
                        ps, bps = getps()
                        for kb in range(kblocks):
                            K.op("pe", lambda e, kb=kb, W=W, M=M, ps=ps: e.matmul(
                                ps[:M, :n], lhsT=W[:, kb, :M], rhs=rhs_fn(kb, t0, n),
                                start=(kb == 0), stop=(kb == kblocks - 1)),
                                reads=[bW] + rhs_bufs, writes=[bps])
                        outs.append((ps, bps, M))
                    consumer(gi, tile, outs)

        stg_ctr = [0]

        with ExitStack() as ph:
            HT, bHT = sb("HT", [128, 16, NT], BF16, ph)
            with ExitStack() as ph2:
                norm_mod(TILES, HT, bHT, GS1, bGS1, SH1, bSH1, ph2)
                K.barrier()
            WP = [sb("WP%d" % i, [128, 16, 128], BF16, ph) for i in range(6)]
            STG = [sb("STG%d" % i, [128, 512], F32, ph) for i in range(4)]
            STB = [sb("STB%d" % i, [128, 512], BF16, ph) for i in range(3)]
            CH, bCH = sb("CH", [128, NT], F32, ph)
            CT, bCT = sb("CT", [128, NT], F32, ph)
            CB, bCB = sb("CB", [128, NT], F32, ph)
            CY, bCY = sb("CY", [128, NT], F32, ph)
            YC, bYC = sb("YC", [128, 4, NT], F32, ph)
            bYCj = [Buf("YC%d" % j) for j in range(4)]
            CW_, bCW_ = sb("CWc", [128, 12], F32, ph)
            MN, bMN = sb("MN", [128, 16], F32, ph)
            K.dma("sp", CW_[:], convw[li], bCW_, writes=[bCW_])
            K.dma("sp", MN[:], mng[li], bMN, writes=[bMN])
            wiv = wview(w_in[li], 16)
            wkv = wview(w_krp[li], 16)
            groups = []
            kinds = []
            for j in range(4):
                groups.append([(wiv, j * 128, 128), (wiv, 1024 + j * 128, 128), (wiv, 512 + j * 128, 128)])
                kinds.append(("conv", j))
            for j in range(4):
                groups.append([(wiv, 1536 + j * 128, 128)])
                kinds.append(("cq", j))
            for j in range(2):
                groups.append([(wiv, 2048 + j * 128, 128)])
                kinds.append(("ckv", j))
            groups.append([(wiv, 2304, 64), (wkv, 0, 64)])
            kinds.append(("kr", 0))
            for j in range(4):
                groups.append([(wiv, 2368 + j * 128, 128)])
                kinds.append(("u", j))

            def store_f32(ps, bps, M, n, dram_ap, dbuf):
                i = stg_ctr[0] % 4
                stg_ctr[0] += 1
                S, bS = STG[i]
                eng = "act" if i % 2 == 0 else "dve"
                if eng == "act":
                    K.op("act", lambda e: e.copy(out=S[:M, :n], in_=ps[:M, :n]), reads=[bps], writes=[bS])
                else:
                    K.op("dve", lambda e: e.tensor_copy(out=S[:M, :n], in_=ps[:M, :n]), reads=[bps], writes=[bS])
                K.dma("sp", dram_ap, S[:M, :n], bS, reads=[bS], writes=[dbuf])

            def conv_finish(j):
                for (a, b) in ((0, CTX), (CTX, NT)):
                    K.op("dve", lambda e, a=a, b=b: e.tensor_scalar(out=CY[:, a:b], in0=CT[:, a:b],
                                                                    scalar1=CW_[:, j * 3 + 1:j * 3 + 2], scalar2=None,
                                                                    op0=ALU.mult), reads=[bCT, bCW_], writes=[bCY])
                    K.op("dve", lambda e, a=a, b=b: e.scalar_tensor_tensor(
                        out=CY[:, a + 1:b], in0=CT[:, a:b - 1], scalar=CW_[:, j * 3:j * 3 + 1], in1=CY[:, a + 1:b],
                        op0=ALU.mult, op1=ALU.add), reads=[bCT, bCW_, bCY], writes=[bCY])
                    K.op("dve", lambda e, a=a, b=b: e.scalar_tensor_tensor(
                        out=CY[:, a:b - 1], in0=CT[:, a + 1:b], scalar=CW_[:, j * 3 + 2:j * 3 + 3], in1=CY[:, a:b - 1],
                        op0=ALU.mult, op1=ALU.add), reads=[bCT, bCW_, bCY], writes=[bCY])
                K.op("pool", lambda e: e.tensor_tensor(out=YC[:, j, :], in0=CY[:], in1=CB[:], op=ALU.mult),
                     reads=[bCY, bCB], writes=[bYCj[j]])

            def consumer(gi, tile, outs):
                (t0, n, r) = tile
                kind, j = kinds[gi]
                if kind == "conv":
                    (p0, b0, _), (p1, b1, _), (p2, b2, _) = outs
                    K.op("act", lambda e: e.copy(out=CH[:, t0:t0 + n], in_=p0[:, :n]), reads=[b0], writes=[bCH])
                    K.op("dve", lambda e: e.tensor_tensor(out=CT[:, t0:t0 + n], in0=p1[:, :n], in1=CH[:, t0:t0 + n],
                                                          op=ALU.mult), reads=[b1, bCH], writes=[bCT])
                    K.op("act", lambda e: e.copy(out=CB[:, t0:t0 + n], in_=p2[:, :n]), reads=[b2], writes=[bCB])
                    if t0 == TILES[-1][0]:
                        conv_finish(j)
                elif kind == "cq":
                    store_f32(outs[0][0], outs[0][1], 128, n, CQ[j * 128:(j + 1) * 128, t0:t0 + n], B_scr["CQ"])
                elif kind == "ckv":
                    store_f32(outs[0][0], outs[0][1], 128, n, CKV[j * 128:(j + 1) * 128, t0:t0 + n], B_scr["CKV"])
                elif kind == "kr":
                    store_f32(outs[0][0], outs[0][1], 64, n, KRs[:, t0:t0 + n], B_scr["KRs"])
                    store_f32(outs[1][0], outs[1][1], 64, n, KRPs[:, t0:t0 + n], B_scr["KRPs"])
                elif kind == "u":
                    store_f32(outs[0][0], outs[0][1], 128, n, Us[j * 128:(j + 1) * 128, t0:t0 + n], B_scr["Us"])

            proj(groups, 16, lambda kb, t0, n: HT[:, kb, t0:t0 + n], [bHT], TILES, consumer, WP)

            def merge_norm(blocks_fn, nb, dim, gain_col0, mix_row0, tiles, SQ, bSQ, RS, bRS, TM):
                for (t0, n, r) in tiles:
                    blocks = blocks_fn(t0, n)
                    ps, bps = sumsq(blocks, n, SQ, bSQ)
                    rstd_from_psum(ps, bps, n, dim, RS, bRS)
                    for j, (ap, bb) in enumerate(blocks):
                        tm, btm = TM[j % 2]
                        K.op("dve", lambda e, ap=ap, tm=tm: e.tensor_tensor(out=tm[:, :n], in0=ap, in1=RS[:, :n], op=ALU.mult),
                             reads=[bb, bRS], writes=[btm])
                        S, bS = STB[j % 3]
                        K.op("act", lambda e, tm=tm, S=S, j=j: e.activation(
                            out=S[:, :n], in_=tm[:, :n], func=AF.Identity,
                            scale=MN[:, gain_col0 + j:gain_col0 + j + 1]), reads=[btm, bMN], writes=[bS])
                        K.dma("sp", MIX[mix_row0 + j * 128:mix_row0 + (j + 1) * 128, t0:t0 + n], S[:, :n], bS,
                              reads=[bS], writes=[B_scr["MIX"]])

            SQ = [None, None]
            bSQ = [None, None]
            for i in range(2):
                SQ[i], bSQ[i] = sb("SQm%d" % i, [128, 512], BF16, ph)
            RS, bRS = sb("RSm", [128, 512], F32, ph)
            TM = [sb("TMm%d" % i, [128, 512], F32, ph) for i in range(2)]
            merge_norm(lambda t0, n: [(YC[:, j, t0:t0 + n], bYCj[j]) for j in range(4)], 4, 512, 0, 0, TILES,
                       SQ, bSQ, RS, bRS, TM)
            K.barrier()

        with ExitStack() as ph:
            CQN, bCQN = sb("CQN", [128, 4, NT], BF16, ph)
            CKN, bCKN = sb("CKN", [128, 2, NT], BF16, ph)
            XB, bXB_ = sb("XBq", [128, 4, 512], F32, ph)
            bXB = [Buf("XBq%d" % k) for k in range(4)]
            SQ = [None, None]
            bSQ = [None, None]
            for i in range(2):
                SQ[i], bSQ[i] = sb("SQq%d" % i, [128, 512], BF16, ph)
            RS, bRS = sb("RSq", [128, 512], F32, ph)
            TM = [sb("TMq%d" % i, [128, 512], F32, ph) for i in range(2)]
            QG, bQG = sb("QG", [128, 4], F32, ph)
            KG, bKG = sb("KG", [128, 2], F32, ph)
            RC, bRC = sb("RC", [64, LAT], F32, ph)
            RSn, bRSn = sb("RSn_", [64, LAT], F32, ph)
            WP = [sb("WQ%d" % i, [128, 4, 128], BF16, ph) for i in range(6)]
            WV, bWV = sb("WV", [128, 2, 1024], BF16, ph)
            STB = [sb("STBq%d" % i, [128, 512], BF16, ph) for i in range(4)]
            M1 = [sb("M1q%d" % i, [64, 512], F32, ph) for i in range(2)]
            M2 = [sb("M2q%d" % i, [64, 512], F32, ph) for i in range(2)]
            KX, bKX = sb("KX", [64, 512], F32, ph)
            KXP, bKXP = sb("KXP", [64, 512], F32, ph)
            K.dma("sp", QG[:], qng[li], bQG, writes=[bQG])
            K.dma("sp", KG[:], kvng[li], bKG, writes=[bKG])
            K.dma("sp", RC[:], ropec[:, :], bRC, writes=[bRC])
            K.dma("sp", RSn[:], ropes[:, :], bRSn, writes=[bRSn])
            K.dma("pool", WV[:], wview(w_v[li], 2), bWV, writes=[bWV])

            def small_norm(src, srcbuf, nb, dim, G, bG, DST, bDST):
                for (t0, n, r) in TILES:
                    for k in range(nb):
                        K.dma("sp", XB[:, k, :n], src[k * 128:(k + 1) * 128, t0:t0 + n], bXB[k],
                              reads=[srcbuf], writes=[bXB[k]])
                    ps, bps = sumsq([(XB[:, k, :n], bXB[k]) for k in range(nb)], n, SQ, bSQ)
                    rstd_from_psum(ps, bps, n, dim, RS, bRS)
                    for k in range(nb):
                        tm, btm = TM[k % 2]
                        K.op("dve", lambda e, k=k, tm=tm: e.tensor_tensor(out=tm[:, :n], in0=XB[:, k, :n], in1=RS[:, :n],
                                                                         op=ALU.mult), reads=[bXB[k], bRS], writes=[btm])
                        K.op("act", lambda e, k=k, tm=tm: e.activation(out=DST[:, k, t0:t0 + n], in_=tm[:, :n], func=AF.Identity,
                                                                      scale=G[:, k:k + 1]), reads=[btm, bG], writes=[bDST])

            small_norm(CQ, B_scr["CQ"], 4, 512, QG, bQG, CQN, bCQN)
            small_norm(CKV, B_scr["CKV"], 2, 256, KG, bKG, CKN, bCKN)

            sctr = [0]

            def store_bf(ps, bps, M, n, dram_ap, dbuf):
                i = sctr[0] % 4
                sctr[0] += 1
                S, bS = STB[i]
                if i % 2 == 0:
                    K.op("act", lambda e: e.copy(out=S[:M, :n], in_=ps[:M, :n]), reads=[bps], writes=[bS])
                else:
                    K.op("dve", lambda e: e.tensor_copy(out=S[:M, :n], in_=ps[:M, :n]), reads=[bps], writes=[bS])
                K.dma("sp", dram_ap, S[:M, :n], bS, reads=[bS], writes=[dbuf])

            def rope_store(x, bx, xp, bxp, t0, n, r, dram_ap, dbuf):
                i = sctr[0] % 4
                sctr[0] += 1
                S, bS = STB[i]
                if r == 1:
                    K.op("dve", lambda e: e.tensor_copy(out=S[:64, :n], in_=x), reads=[bx], writes=[bS])
                else:
                    l0 = t0 - CTX
                    m1, bm1 = M1[i % 2]
                    m2, bm2 = M2[i % 2]
                    K.op("dve", lambda e: e.tensor_tensor(out=m1[:, :n], in0=x, in1=RC[:, l0:l0 + n], op=ALU.mult),
                         reads=[bx, bRC], writes=[bm1])
                    K.op("dve", lambda e: e.tensor_tensor(out=m2[:, :n], in0=xp, in1=RSn[:, l0:l0 + n], op=ALU.mult),
                         reads=[bxp, bRSn], writes=[bm2])
                    K.op("pool", lambda e: e.tensor_tensor(out=S[:64, :n], in0=m1[:, :n], in1=m2[:, :n], op=ALU.add),
                         reads=[bm1, bm2], writes=[bS])
                K.dma("sp", dram_ap, S[:64, :n], bS, reads=[bS], writes=[dbuf])

            wuv = wview(w_uq[li], 4)
            wupv = wview(w_uqp[li], 4)
            groups = [[(wuv, 192 * h, 128), (wuv, 192 * h + 128, 64), (wupv, 64 * h, 64)] for h in range(8)]

            def cons_q(gi, tile, outs):
                (t0, n, r) = tile
                h = gi
                store_bf(outs[0][0], outs[0][1], 128, n, QN[h * 128:(h + 1) * 128, t0:t0 + n], B_scr["QN"])
                rope_store(outs[1][0][:64, :n], outs[1][1], outs[2][0][:64, :n], outs[2][1], t0, n, r,
                           QR[h * 64:(h + 1) * 64, t0:t0 + n], B_scr["QR"])

            proj(groups, 4, lambda kb, t0, n: CQN[:, kb, t0:t0 + n], [bCQN], TILES, cons_q, WP)

            wkvv = wview(w_ukv[li], 2)
            groups = [[(wkvv, 256 * h, 128)] for h in range(8)]

            def cons_k(gi, tile, outs):
                (t0, n, r) = tile
                store_bf(outs[0][0], outs[0][1], 128, n, KN[gi * 128:(gi + 1) * 128, t0:t0 + n], B_scr["KN"])

            proj(groups, 2, lambda kb, t0, n: CKN[:, kb, t0:t0 + n], [bCKN], TILES, cons_k, WP)

            for tt in range(NT // 128):
                for hf in range(2):
                    ps, bps = getps()
                    for kb in range(2):
                        K.op("pe", lambda e, kb=kb, ps=ps: e.matmul(ps[:, :], lhsT=CKN[:, kb, tt * 128:(tt + 1) * 128],
                                                                    rhs=WV[:, kb, hf * 512:(hf + 1) * 512],
                                                                    start=(kb == 0), stop=(kb == 1)),
                             reads=[bCKN, bWV], writes=[bps])
                    store_bf(ps, bps, 128, 512, Vs[tt * 128:(tt + 1) * 128, hf * 512:(hf + 1) * 512], B_scr["Vs"])

            for (t0, n, r) in TILES:
                K.dma("sp", KX[:, :n], KRs[:, t0:t0 + n], bKX, reads=[B_scr["KRs"]], writes=[bKX])
                K.dma("sp", KXP[:, :n], KRPs[:, t0:t0 + n], bKXP, reads=[B_scr["KRPs"]], writes=[bKXP])
                rope_store(KX[:, :n], bKX, KXP[:, :n], bKXP, t0, n, r, KRR[:, t0:t0 + n], B_scr["KRR"])
            K.barrier()

        with ExitStack() as ph:
            KNh = [sb("KNh%d" % i, [128, NT], BF16, ph) for i in range(2)]
            QNh = [sb("QNh%d" % i, [128, NT], BF16, ph) for i in range(2)]
            QRh = [sb("QRh%d" % i, [64, NT], BF16, ph) for i in range(2)]
            Vh = [sb("Vh%d" % i, [128, 18, 128], BF16, ph) for i in range(2)]
            KRh, bKRh = sb("KRh", [64, NT], BF16, ph)
            PT = [sb("PT%d" % i, [128, 512], BF16, ph) for i in range(3)]
            RD, bRD = sb("RD", [128, 512], F32, ph)
            OS = [sb("OS%d" % i, [128, 512], F32, ph) for i in range(2)]
            K.dma("sp", KRh[:], KRR[:, :], bKRh, reads=[B_scr["KRR"]], writes=[bKRh])
            vview = Vs.rearrange("(kb p) d -> p kb d", p=128)
            octr = 0
            for h in range(8):
                kn, bkn = KNh[h % 2]
                qn, bqn = QNh[h % 2]
                qr, bqr = QRh[h % 2]
                vh, bvh = Vh[h % 2]
                K.dma("sp", kn[:], KN[h * 128:(h + 1) * 128, :], bkn, reads=[B_scr["KN"]], writes=[bkn])
                K.dma("sp", qn[:], QN[h * 128:(h + 1) * 128, :], bqn, reads=[B_scr["QN"]], writes=[bqn])
                K.dma("sp", qr[:], QR[h * 64:(h + 1) * 64, :], bqr, reads=[B_scr["QR"]], writes=[bqr])
                K.dma("sp", vh[:], vview[:, :, h * 128:(h + 1) * 128], bvh, reads=[B_scr["Vs"]], writes=[bvh])
                for (t0, n, r) in tiles_out:
                    nkb = 2 if r == 1 else 18
                    po, bpo = getps(0, 4)
                    pd, bpd = getps(0, 4)
                    pend = None
                    for kb in range(nkb + 1):
                        cur = None
                        if kb < nkb:
                            ps_, bps_ = getps(4, 8)
                            ks = slice(kb * 128, (kb + 1) * 128)
                            K.op("pe", lambda e, ps_=ps_, ks=ks: e.matmul(ps_[:, :n], lhsT=kn[:, ks], rhs=qn[:, t0:t0 + n],
                                                                          start=True, stop=False),
                                 reads=[bkn, bqn], writes=[bps_])
                            K.op("pe", lambda e, ps_=ps_, ks=ks: e.matmul(ps_[:, :n], lhsT=KRh[:, ks], rhs=qr[:, t0:t0 + n],
                                                                          start=False, stop=True),
                                 reads=[bKRh, bqr], writes=[bps_])
                            pt, bpt = PT[kb % 3]
                            K.op("act", lambda e, ps_=ps_, pt=pt: e.activation(out=pt[:, :n], in_=ps_[:, :n], func=AF.Exp,
                                                                               scale=MLA_SCALE),
                                 reads=[bps_], writes=[bpt])
                            cur = (kb, pt, bpt)
                        if pend is not None:
                            pk, ppt, pbpt = pend
                            K.op("pe", lambda e, pk=pk, ppt=ppt: e.matmul(po[:, :n], lhsT=vh[:, pk, :], rhs=ppt[:, :n],
                                                                          start=(pk == 0), stop=(pk == nkb - 1)),
                                 reads=[bvh, pbpt], writes=[bpo])
                            K.op("pe", lambda e, pk=pk, ppt=ppt: e.matmul(pd[:, :n], lhsT=ONES[:], rhs=ppt[:, :n],
                                                                          start=(pk == 0), stop=(pk == nkb - 1)),
                                 reads=[bONES, pbpt], writes=[bpd])
                        pend = cur
                    K.op("dve", lambda e: e.reciprocal(out=RD[:, :n], in_=pd[:, :n]), reads=[bpd], writes=[bRD])
                    os_, bos = OS[octr % 2]
                    octr += 1
                    K.op("dve", lambda e, os_=os_: e.tensor_tensor(out=os_[:, :n], in0=po[:, :n], in1=RD[:, :n], op=ALU.mult),
                         reads=[bpo, bRD], writes=[bos])
                    K.dma("sp", ATT[h * 128:(h + 1) * 128, t0:t0 + n], os_[:, :n], bos, reads=[bos], writes=[B_scr["ATT"]])
            K.barrier()

        with ExitStack() as ph:
            UT, bUT = sb("UT", [128, 4, NT], BF16, ph)
            YACC, bYACC_ = sb("YACC", [128, 4, NT], F32, ph)
            bYA = [Buf("YA%d" % c) for c in range(4)]
            ph2 = ExitStack()
            _ph_outer = ph
            ph = ph2
            BW, bBW = sb("BWs", [128, 2 * 2 * 16 * 128], BF16, ph)
            CWf, bCWf = sb("CWf", [128, 2 * 16 * 128], F32, ph)
            CWb, bCWb = sb("CWb", [128, 2 * 16 * 128], BF16, ph)
            SA, bSA = sb("SA", [128, 96], F32, ph)
            IOT, bIOT = sb("IOT", [128, 256], F32, ph)
            TABC, bTABC = sb("TABC", [128, 16, 256], F32, ph)
            TABS, bTABS = sb("TABS", [128, 16, 256], F32, ph)
            RHOT, bRHOT = sb("RHOT", [128, 16, 256], F32, ph)
            ANG, bANG = sb("ANG", [128, 256], F32, ph)
            ANG2, bANG2 = sb("ANG2", [128, 256], F32, ph)
            ANG3, bANG3 = sb("ANG3", [128, 256], F32, ph)
            sm = {}
            for nm in ["DT", "LR", "LI", "RHO", "ER", "EI", "NEI", "AR", "AI", "T1", "T2", "T3", "DEN", "KR_", "KI_", "NKR", "NKI",
                       "CAR", "CAI", "ABR", "ABI", "TL1", "TL2", "AE", "AE2", "AE3"]:
                sm[nm] = sb("s5_" + nm, [128, 16], F32, ph)
            mt = {}
            for nm in ["m1", "m2", "m3", "m4", "gir", "gii", "GR", "GI", "m5", "m6", "m7", "m8"]:
                mt[nm] = sb("s5t_" + nm, [128, 256], F32, ph)
            HRt = [sb("HRt%d" % i, [128, 256], BF16, ph) for i in range(2)]
            HIt = [sb("HIt%d" % i, [128, 256], BF16, ph) for i in range(2)]
            XU, bXU = sb("XU", [128, 512], F32, ph)
            K.dma("sp", IOT[:], iota_in[:, :], bIOT, writes=[bIOT])
            K.dma("sp", SA[:], ssa[li], bSA, writes=[bSA])
            K.dma("pool", BW[:], bw_in[li], bBW, writes=[bBW])
            for cb in range(4):
                for (t0, n, r) in TILES:
                    K.dma("sp", XU[:, :n], Us[cb * 128:(cb + 1) * 128, t0:t0 + n], bXU, reads=[B_scr["Us"]], writes=[bXU])
                    K.op("act", lambda e, cb=cb, t0=t0, n=n: e.copy(out=UT[:, cb, t0:t0 + n], in_=XU[:, :n]),
                         reads=[bXU], writes=[bUT])

            def S(nm):
                return sm[nm][0]

            def bS_(nm):
                return sm[nm][1]

            def vop(fn, reads, writes):
                K.op("dve", fn, reads=[bS_(x) if isinstance(x, str) else x for x in reads],
                     writes=[bS_(x) if isinstance(x, str) else x for x in writes])

            MAGIC = 12582912.0
            INV2PI = 1.0 / TWO_PI

            def sincos(angle_t, bang, out_sin, bsin, out_cos, bcos, shape_cols):
                tmp, btmp = (ANG2, bANG2) if shape_cols == 256 else sm["AE2"]
                uu, buu = (ANG3, bANG3) if shape_cols == 256 else sm["AE3"]
                c = shape_cols
                for (dst, bdst, off) in ((out_sin, bsin, 0.0), (out_cos, bcos, math.pi / 2)):
                    K.op("dve", lambda e, off=off: e.tensor_scalar(out=uu[:, :c], in0=angle_t, scalar1=off, scalar2=None, op0=ALU.add),
                         reads=[bang], writes=[buu])
                    K.op("dve", lambda e: e.tensor_scalar(out=tmp[:, :c], in0=uu[:, :c], scalar1=INV2PI, scalar2=MAGIC,
                                                          op0=ALU.mult, op1=ALU.add), reads=[buu], writes=[btmp])
                    K.op("dve", lambda e: e.tensor_scalar(out=tmp[:, :c], in0=tmp[:, :c], scalar1=-MAGIC, scalar2=TWO_PI,
                                                          op0=ALU.add, op1=ALU.mult), reads=[btmp], writes=[btmp])
                    K.op("dve", lambda e: e.tensor_tensor(out=uu[:, :c], in0=uu[:, :c], in1=tmp[:, :c], op=ALU.subtract),
                         reads=[buu, btmp], writes=[buu])
                    K.op("dve", lambda e: e.tensor_scalar(out=uu[:, :c], in0=uu[:, :c], scalar1=-3.1415925, scalar2=3.1415925,
                                                          op0=ALU.max, op1=ALU.min), reads=[buu], writes=[buu])
                    K.op("act", lambda e, dst=dst: e.activation(out=dst, in_=uu[:, :c], func=AF.Sin),
                         reads=[buu], writes=[bdst])

            for d in range(2):
                ar = SA[:, (d * 3 + 0) * 16:(d * 3 + 0) * 16 + 16]
                ai = SA[:, (d * 3 + 1) * 16:(d * 3 + 1) * 16 + 16]
                ldt = SA[:, (d * 3 + 2) * 16:(d * 3 + 2) * 16 + 16]
                K.op("act", lambda e: e.activation(out=S("DT")[:], in_=ldt, func=AF.Exp), reads=[bSA], writes=[bS_("DT")])
                vop(lambda e: e.tensor_tensor(out=S("LR")[:], in0=ar, in1=S("DT")[:], op=ALU.mult), [bSA, "DT"], ["LR"])
                vop(lambda e: e.tensor_tensor(out=S("LI")[:], in0=ai, in1=S("DT")[:], op=ALU.mult), [bSA, "DT"], ["LI"])
                K.op("act", lambda e: e.activation(out=S("RHO")[:], in_=S("LR")[:], func=AF.Exp), reads=[bS_("LR")], writes=[bS_("RHO")])
                sincos(S("LI")[:], bS_("LI"), S("T1")[:], bS_("T1"), S("T2")[:], bS_("T2"), 16)
                vop(lambda e: e.tensor_tensor(out=S("ABR")[:], in0=S("RHO")[:], in1=S("T2")[:], op=ALU.mult), ["RHO", "T2"], ["ABR"])
                vop(lambda e: e.tensor_tensor(out=S("ABI")[:], in0=S("RHO")[:], in1=S("T1")[:], op=ALU.mult), ["RHO", "T1"], ["ABI"])
                vop(lambda e: e.tensor_scalar(out=S("T3")[:], in0=S("ABR")[:], scalar1=-1.0, scalar2=None, op0=ALU.add), ["ABR"], ["T3"])
                vop(lambda e: e.tensor_tensor(out=S("TL1")[:], in0=ar, in1=ar, op=ALU.mult), [bSA], ["TL1"])
                vop(lambda e: e.tensor_tensor(out=S("TL2")[:], in0=ai, in1=ai, op=ALU.mult), [bSA], ["TL2"])
                vop(lambda e: e.tensor_tensor(out=S("DEN")[:], in0=S("TL1")[:], in1=S("TL2")[:], op=ALU.add), ["TL1", "TL2"], ["DEN"])
                vop(lambda e: e.reciprocal(out=S("DEN")[:], in_=S("DEN")[:]), ["DEN"], ["DEN"])
                vop(lambda e: e.tensor_tensor(out=S("TL1")[:], in0=S("T3")[:], in1=ar, op=ALU.mult), ["T3", bSA], ["TL1"])
                vop(lambda e: e.tensor_tensor(out=S("TL2")[:], in0=S("ABI")[:], in1=ai, op=ALU.mult), ["ABI", bSA], ["TL2"])
                vop(lambda e: e.tensor_tensor(out=S("KR_")[:], in0=S("TL1")[:], in1=S("TL2")[:], op=ALU.add), ["TL1", "TL2"], ["KR_"])
                vop(lambda e: e.tensor_tensor(out=S("KR_")[:], in0=S("KR_")[:], in1=S("DEN")[:], op=ALU.mult), ["KR_", "DEN"], ["KR_"])
                vop(lambda e: e.tensor_tensor(out=S("TL1")[:], in0=S("ABI")[:], in1=ar, op=ALU.mult), ["ABI", bSA], ["TL1"])
                vop(lambda e: e.tensor_tensor(out=S("TL2")[:], in0=S("T3")[:], in1=ai, op=ALU.mult), ["T3", bSA], ["TL2"])
                vop(lambda e: e.tensor_tensor(out=S("KI_")[:], in0=S("TL1")[:], in1=S("TL2")[:], op=ALU.subtract), ["TL1", "TL2"], ["KI_"])
                vop(lambda e: e.tensor_tensor(out=S("KI_")[:], in0=S("KI_")[:], in1=S("DEN")[:], op=ALU.mult), ["KI_", "DEN"], ["KI_"])
                vop(lambda e: e.tensor_scalar(out=S("NKR")[:], in0=S("KR_")[:], scalar1=-1.0, scalar2=None, op0=ALU.mult), ["KR_"], ["NKR"])
                vop(lambda e: e.tensor_scalar(out=S("NKI")[:], in0=S("KI_")[:], scalar1=-1.0, scalar2=None, op0=ALU.mult), ["KI_"], ["NKI"])
                vop(lambda e: e.tensor_scalar(out=S("AE")[:], in0=S("LI")[:], scalar1=256.0, scalar2=None, op0=ALU.mult), ["LI"], ["AE"])
                sincos(S("AE")[:], bS_("AE"), S("EI")[:], bS_("EI"), S("ER")[:], bS_("ER"), 16)
                vop(lambda e: e.tensor_scalar(out=S("NEI")[:], in0=S("EI")[:], scalar1=-1.0, scalar2=None, op0=ALU.mult), ["EI"], ["NEI"])
                vop(lambda e: e.memset(S("CAR")[:], 0.0), [], ["CAR"])
                vop(lambda e: e.memset(S("CAI")[:], 0.0), [], ["CAI"])
                K.dma("sp", CWf[:], cw_in[li][:, d * 4096:(d + 1) * 4096], bCWf, writes=[bCWf])
                for rb in range(16):
                    K.op("dve", lambda e, rb=rb: e.tensor_scalar(out=ANG[:], in0=IOT[:], scalar1=S("LI")[:, rb:rb + 1], scalar2=None,
                                                                 op0=ALU.mult), reads=[bIOT, bS_("LI")], writes=[bANG])
                    sincos(ANG[:], bANG, TABS[:, rb, :], bTABS, TABC[:, rb, :], bTABC, 256)
                    K.op("dve", lambda e, rb=rb: e.tensor_scalar(out=RHOT[:, rb, :], in0=IOT[:], scalar1=0.0,
                                                                  scalar2=S("RHO")[:, rb:rb + 1], op0=ALU.mult, op1=ALU.add),
                         reads=[bIOT, bS_("RHO")], writes=[bRHOT])
                    cre = CWf[:, rb * 128:(rb + 1) * 128]
                    cim = CWf[:, 2048 + rb * 128:2048 + (rb + 1) * 128]
                    t1_, bt1_ = mt["m1"]
                    t2_, bt2_ = mt["m2"]
                    K.op("dve", lambda e, rb=rb, cre=cre: e.tensor_scalar(out=t1_[:, :128], in0=cre, scalar1=S("KR_")[:, rb:rb + 1],
                                                                          scalar2=None, op0=ALU.mult),
                         reads=[bCWf, bS_("KR_")], writes=[bt1_])
                    K.op("dve", lambda e, rb=rb, cim=cim: e.scalar_tensor_tensor(
                        out=CWb[:, rb * 128:(rb + 1) * 128], in0=cim, scalar=S("NKI")[:, rb:rb + 1], in1=t1_[:, :128],
                        op0=ALU.mult, op1=ALU.add), reads=[bCWf, bS_("NKI"), bt1_], writes=[bCWb])
                    K.op("dve", lambda e, rb=rb, cre=cre: e.tensor_scalar(out=t2_[:, :128], in0=cre, scalar1=S("NKI")[:, rb:rb + 1],
                                                                          scalar2=None, op0=ALU.mult),
                         reads=[bCWf, bS_("NKI")], writes=[bt2_])
                    K.op("dve", lambda e, rb=rb, cim=cim: e.scalar_tensor_tensor(
                        out=CWb[:, 2048 + rb * 128:2048 + (rb + 1) * 128], in0=cim, scalar=S("NKR")[:, rb:rb + 1], in1=t2_[:, :128],
                        op0=ALU.mult, op1=ALU.add), reads=[bCWf, bS_("NKR"), bt2_], writes=[bCWb])
                order = list(range(9)) if d == 0 else [0] + list(range(8, 0, -1))
                rev = (d == 1)
                hctr = 0
                for ck in order:
                    c0 = ck * 256
                    for cb in range(4):
                        py, bpy = getps(0, 2)
                        for rbl in range(4):
                            rb = cb * 4 + rbl
                            pa, bpa = getps(2, 8)
                            pb, bpb = getps(2, 8)
                            bwr = BW[:, ((d * 2 + 0) * 16 + rb) * 128:((d * 2 + 0) * 16 + rb + 1) * 128]
                            bwi = BW[:, ((d * 2 + 1) * 16 + rb) * 128:((d * 2 + 1) * 16 + rb + 1) * 128]
                            K.op("pe", lambda e, pa=pa, bwr=bwr: e.matmul(pa[:, :256], lhsT=bwr, rhs=UT[:, cb, c0:c0 + 256],
                                                                          start=True, stop=True), reads=[bBW, bUT], writes=[bpa])
                            K.op("pe", lambda e, pb=pb, bwi=bwi: e.matmul(pb[:, :256], lhsT=bwi, rhs=UT[:, cb, c0:c0 + 256],
                                                                          start=True, stop=True), reads=[bBW, bUT], writes=[bpb])
                            X = pa[:, 255::-1] if rev else pa[:, :256]
                            Y = pb[:, 255::-1] if rev else pb[:, :256]
                            tc_ = TABC[:, rb, :]
                            ts_ = TABS[:, rb, :]

                            def tt(eng, o, a, b_, op, rd, wr):
                                K.op(eng, lambda e: e.tensor_tensor(out=o, in0=a, in1=b_, op=op), reads=rd, writes=wr)

                            tt("dve", mt["m1"][0][:], X, tc_, ALU.mult, [bpa, bTABC], [mt["m1"][1]])
                            tt("dve", mt["m2"][0][:], Y, ts_, ALU.mult, [bpb, bTABS], [mt["m2"][1]])
                            tt("pool", mt["gir"][0][:], mt["m1"][0][:], mt["m2"][0][:], ALU.add, [mt["m1"][1], mt["m2"][1]], [mt["gir"][1]])
                            tt("dve", mt["m3"][0][:], Y, tc_, ALU.mult, [bpb, bTABC], [mt["m3"][1]])
                            tt("dve", mt["m4"][0][:], X, ts_, ALU.mult, [bpa, bTABS], [mt["m4"][1]])
                            tt("pool", mt["gii"][0][:], mt["m3"][0][:], mt["m4"][0][:], ALU.subtract, [mt["m3"][1], mt["m4"][1]], [mt["gii"][1]])
                            K.op("dve", lambda e, rb=rb: e.tensor_tensor_scan(out=mt["GR"][0][:], data0=RHOT[:, rb, :], data1=mt["gir"][0][:],
                                                                              initial=S("CAR")[:, rb:rb + 1], op0=ALU.mult, op1=ALU.add),
                                 reads=[bRHOT, mt["gir"][1], bS_("CAR")], writes=[mt["GR"][1]])
                            K.op("dve", lambda e, rb=rb: e.tensor_tensor_scan(out=mt["GI"][0][:], data0=RHOT[:, rb, :], data1=mt["gii"][0][:],
                                                                              initial=S("CAI")[:, rb:rb + 1], op0=ALU.mult, op1=ALU.add),
                                 reads=[bRHOT, mt["gii"][1], bS_("CAI")], writes=[mt["GI"][1]])
                            GRl = mt["GR"][0][:, 255:256]
                            GIl = mt["GI"][0][:, 255:256]
                            K.op("dve", lambda e, rb=rb: e.tensor_scalar(out=S("T1")[:, rb:rb + 1], in0=GRl, scalar1=S("ER")[:, rb:rb + 1],
                                                                          scalar2=None, op0=ALU.mult),
                                 reads=[mt["GR"][1], bS_("ER")], writes=[bS_("T1")])
                            K.op("dve", lambda e, rb=rb: e.scalar_tensor_tensor(out=S("CAR")[:, rb:rb + 1], in0=GIl, scalar=S("NEI")[:, rb:rb + 1],
                                                                                 in1=S("T1")[:, rb:rb + 1], op0=ALU.mult, op1=ALU.add),
                                 reads=[mt["GI"][1], bS_("NEI"), bS_("T1")], writes=[bS_("CAR")])
                            K.op("dve", lambda e, rb=rb: e.tensor_scalar(out=S("T2")[:, rb:rb + 1], in0=GIl, scalar1=S("ER")[:, rb:rb + 1],
                                                                          scalar2=None, op0=ALU.mult),
                                 reads=[mt["GI"][1], bS_("ER")], writes=[bS_("T2")])
                            K.op("dve", lambda e, rb=rb: e.scalar_tensor_tensor(out=S("CAI")[:, rb:rb + 1], in0=GRl, scalar=S("EI")[:, rb:rb + 1],
                                                                                 in1=S("T2")[:, rb:rb + 1], op0=ALU.mult, op1=ALU.add),
                                 reads=[mt["GR"][1], bS_("EI"), bS_("T2")], writes=[bS_("CAI")])
                            hr, bhr = HRt[hctr % 2]
                            hi, bhi = HIt[hctr % 2]
                            hctr += 1
                            HRo = hr[:, 255::-1] if rev else hr[:, :]
                            HIo = hi[:, 255::-1] if rev else hi[:, :]
                            tt("dve", mt["m5"][0][:], mt["GR"][0][:], tc_, ALU.mult, [mt["GR"][1], bTABC], [mt["m5"][1]])
                            tt("dve", mt["m6"][0][:], mt["GI"][0][:], ts_, ALU.mult, [mt["GI"][1], bTABS], [mt["m6"][1]])
                            tt("pool", HRo, mt["m5"][0][:], mt["m6"][0][:], ALU.subtract, [mt["m5"][1], mt["m6"][1]], [bhr])
                            tt("dve", mt["m7"][0][:], mt["GI"][0][:], tc_, ALU.mult, [mt["GI"][1], bTABC], [mt["m7"][1]])
                            tt("dve", mt["m8"][0][:], mt["GR"][0][:], ts_, ALU.mult, [mt["GR"][1], bTABS], [mt["m8"][1]])
                            tt("pool", HIo, mt["m7"][0][:], mt["m8"][0][:], ALU.add, [mt["m7"][1], mt["m8"][1]], [bhi])
                            K.op("pe", lambda e, rb=rb, hr=hr: e.matmul(py[:, :256], lhsT=CWb[:, rb * 128:(rb + 1) * 128], rhs=hr[:, :],
                                                                        start=(rbl == 0), stop=False), reads=[bCWb, bhr], writes=[bpy])
                            K.op("pe", lambda e, rb=rb, hi=hi: e.matmul(py[:, :256], lhsT=CWb[:, 2048 + rb * 128:2048 + (rb + 1) * 128],
                                                                        rhs=hi[:, :], start=False, stop=(rbl == 3)),
                                 reads=[bCWb, bhi], writes=[bpy])
                        if d == 0:
                            K.op("act", lambda e, cb=cb: e.copy(out=YACC[:, cb, c0:c0 + 256], in_=py[:, :256]),
                                 reads=[bpy], writes=[bYA[cb]])
                        else:
                            K.op("dve", lambda e, cb=cb: e.tensor_tensor(out=YACC[:, cb, c0:c0 + 256], in0=py[:, :256],
                                                                         in1=YACC[:, cb, c0:c0 + 256], op=ALU.add),
                                 reads=[bpy, bYA[cb]], writes=[bYA[cb]])
            K.barrier()
            ph2.close()
            ph = _ph_outer
            XU, bXU = sb("XU2", [128, 512], F32, ph)
            TA, bTA = sb("TA5", [128, 256], F32, ph)
            SD, bSD = sb("SD", [128, 4], F32, ph)
            BG, bBG = sb("BG", [128, 4], F32, ph)
            MN, bMN = sb("MN5", [128, 16], F32, ph)
            K.dma("sp", SD[:], ssd[li], bSD, writes=[bSD])
            K.dma("sp", BG[:], bglu[li], bBG, writes=[bBG])
            K.dma("sp", MN[:], mng[li], bMN, writes=[bMN])
            GB, bGB = UT, bUT
            WG = [sb("WG%d" % i, [128, 4, 128], BF16, ph) for i in range(6)]
            for cb in range(4):
                for (t0, n, r) in TILES:
                    for q in range(0, n, 256):
                        c0 = t0 + q
                        ya = YACC[:, cb, c0:c0 + 256]
                        K.dma("sp", XU[:, :256], Us[cb * 128:(cb + 1) * 128, c0:c0 + 256], bXU, reads=[B_scr["Us"]], writes=[bXU])
                        K.op("dve", lambda e, ya=ya, cb=cb: e.scalar_tensor_tensor(out=ya, in0=XU[:, :256], scalar=SD[:, cb:cb + 1], in1=ya,
                                                                                   op0=ALU.mult, op1=ALU.add),
                             reads=[bXU, bSD, bYA[cb]], writes=[bYA[cb]])
                        K.op("dve", lambda e, ya=ya: e.tensor_tensor(out=TA[:], in0=ya, in1=ya, op=ALU.mult), reads=[bYA[cb]], writes=[bTA])
                        K.op("dve", lambda e: e.tensor_scalar(out=TA[:], in0=TA[:], scalar1=0.044715, scalar2=1.0, op0=ALU.mult, op1=ALU.add),
                             reads=[bTA], writes=[bTA])
                        K.op("dve", lambda e, ya=ya: e.tensor_tensor(out=TA[:], in0=TA[:], in1=ya, op=ALU.mult), reads=[bTA, bYA[cb]], writes=[bTA])
                        K.op("act", lambda e: e.activation(out=TA[:], in_=TA[:], func=AF.Sigmoid, scale=1.5957691216057308),
                             reads=[bTA], writes=[bTA])
                        K.op("dve", lambda e, ya=ya: e.tensor_tensor(out=ya, in0=ya, in1=TA[:], op=ALU.mult), reads=[bTA, bYA[cb]], writes=[bYA[cb]])
                        K.op("act", lambda e, ya=ya, cb=cb, c0=c0: e.copy(out=GB[:, cb, c0:c0 + 256], in_=ya), reads=[bYA[cb]], writes=[bGB])
            wgv = wview(w_glu[li], 4)
            groups = [[(wgv, j * 128, 128)] for j in range(4)]
            YS, bYS_ = sb("YS", [128, 4, NT], F32, ph)
            bYSj = [Buf("YS%d" % j) for j in range(4)]
            SGt = [sb("SGt%d" % i, [128, 512], F32, ph) for i in range(2)]

            def cons_g(gi, tile, outs):
                (t0, n, r) = tile
                sg, bsg = SGt[gi % 2]
                ps, bps, M = outs[0]
                K.op("act", lambda e: e.activation(out=sg[:, :n], in_=ps[:, :n], func=AF.Sigmoid, bias=BG[:, gi:gi + 1]),
                     reads=[bps, bBG], writes=[bsg])
                K.op("dve", lambda e: e.tensor_tensor(out=YS[:, gi, t0:t0 + n], in0=YACC[:, gi, t0:t0 + n], in1=sg[:, :n], op=ALU.mult),
                     reads=[bsg, bYA[gi]], writes=[bYSj[gi]])

            proj(groups, 4, lambda kb, t0, n: GB[:, kb, t0:t0 + n], [bGB], TILES, cons_g, WG)
            STB = [sb("STB5%d" % i, [128, 512], BF16, ph) for i in range(3)]
            SQ = [None, None]
            bSQ = [None, None]
            for i in range(2):
                SQ[i], bSQ[i] = sb("SQ5%d" % i, [128, 512], BF16, ph)
            RS, bRS = sb("RS5", [128, 512], F32, ph)
            TM = [sb("TM5%d" % i, [128, 512], F32, ph) for i in range(2)]
            merge_norm(lambda t0, n: [(YS[:, j, t0:t0 + n], bYSj[j]) for j in range(4)], 4, 512, 12, 1536, TILES,
                       SQ, bSQ, RS, bRS, TM)
            K.barrier()

        with ExitStack() as ph:
            AX, bAX_ = sb("AX", [128, 8, 512], F32, ph)
            bAX = [Buf("AX%d" % j) for j in range(8)]
            MN, bMN = sb("MNa", [128, 16], F32, ph)
            K.dma("sp", MN[:], mng[li], bMN, writes=[bMN])
            STB = [sb("STBa%d" % i, [128, 512], BF16, ph) for i in range(3)]
            SQ = [None, None]
            bSQ = [None, None]
            for i in range(2):
                SQ[i], bSQ[i] = sb("SQa%d" % i, [128, 512], BF16, ph)
            RS, bRS = sb("RSa", [128, 512], F32, ph)
            TM = [sb("TMa%d" % i, [128, 512], F32, ph) for i in range(2)]

            def att_blocks(t0, n):
                for j in range(8):
                    K.dma("sp", AX[:, j, :n], ATT[j * 128:(j + 1) * 128, t0:t0 + n], bAX[j], reads=[B_scr["ATT"]], writes=[bAX[j]])
                return [(AX[:, j, :n], bAX[j]) for j in range(8)]

            merge_norm(att_blocks, 8, 1024, 4, 512, tiles_out, SQ, bSQ, RS, bRS, TM)
            K.barrier()

        with ExitStack() as ph:
            MT, bMT_ = sb("MT", [128, 16, NT], BF16, ph)
            bMTk = [Buf("MT%d" % k) for k in range(16)]
            for kb in range(16):
                K.dma("sp", MT[:, kb, :], MIX[kb * 128:(kb + 1) * 128, :], bMTk[kb], reads=[B_scr["MIX"]], writes=[bMTk[kb]])
            WP = [sb("WO%d" % i, [128, 16, 128], BF16, ph) for i in range(6)]
            XI = [sb("XI%d" % i, [128, 512], F32, ph) for i in range(3)]
            wov = wview(w_o[li], 16)
            groups = [[(wov, j * 128, 128)] for j in range(16)]
            xctr = [0]

            def cons_o(gi, tile, outs):
                (t0, n, r) = tile
                xi, bxi = XI[xctr[0] % 3]
                xctr[0] += 1
                ps, bps, M = outs[0]
                K.dma("sp", xi[:, :n], XR[gi * 128:(gi + 1) * 128, t0:t0 + n], bxi, reads=[B_XR[gi]], writes=[bxi])
                K.op("dve", lambda e: e.scalar_tensor_tensor(out=xi[:, :n], in0=ps[:, :n], scalar=GG1[:, r * 16 + gi:r * 16 + gi + 1],
                                                             in1=xi[:, :n], op0=ALU.mult, op1=ALU.add),
                     reads=[bps, bGG1, bxi], writes=[bxi])
                K.dma("sp", XR[gi * 128:(gi + 1) * 128, t0:t0 + n], xi[:, :n], bxi, reads=[bxi], writes=[B_XR[gi]])

            proj(groups, 16, lambda kb, t0, n: MT[:, kb, t0:t0 + n], bMTk, tiles_out, cons_o, WP)
            K.barrier()

        tgroups = [[TILES[1], TILES[2]], [TILES[3], TILES[4]]] if is_last else [[TILES[0], TILES[1]], [TILES[2], TILES[3]], [TILES[4]]]
        for tg in tgroups:
            g0 = tg[0][0]
            gn = sum(t[1] for t in tg)
            with ExitStack() as ph:
                H2, bH2 = sb("H2", [128, 16, gn], BF16, ph)
                with ExitStack() as ph2:
                    norm_mod(tg, H2, bH2, GS2, bGS2, SH2, bSH2, ph2, dst_off=g0)
                    K.barrier()
                HID, bHID_ = sb("HID", [128, 44, gn], BF16, ph)
                bHIDj = [Buf("HID%d" % j) for j in range(44)]
                WP = [sb("WF%d" % i, [128, 16, 128], BF16, ph) for i in range(6)]
                S1 = [sb("S1f%d" % i, [128, 512], F32, ph) for i in range(2)]
                wgv_ = wview(w_gate[li], 16)
                wuv_ = wview(w_up[li], 16)
                groups = [[(wgv_, j * 128, 128), (wuv_, j * 128, 128)] for j in range(44)]

                def cons_f1(gi, tile, outs):
                    (t0, n, r) = tile
                    s1, bs1 = S1[gi % 2]
                    K.op("act", lambda e: e.activation(out=s1[:, :n], in_=outs[0][0][:, :n], func=AF.Silu),
                         reads=[outs[0][1]], writes=[bs1])
                    K.op("dve", lambda e: e.tensor_tensor(out=HID[:, gi, t0 - g0:t0 - g0 + n], in0=outs[1][0][:, :n], in1=s1[:, :n],
                                                          op=ALU.mult), reads=[outs[1][1], bs1], writes=[bHIDj[gi]])

                proj(groups, 16, lambda kb, t0, n: H2[:, kb, t0 - g0:t0 - g0 + n], [bH2], tg, cons_f1, WP)
                WD = [sb("WD%d" % i, [128, 44, 128], BF16, ph) for i in range(2)]
                XI = [sb("XIf%d" % i, [128, 512], F32, ph) for i in range(3)]
                wdv = wview(w_down[li], 44)
                xctr = [0]
                for nb_ in range(16):
                    W, bW = WD[nb_ % 2]
                    K.dma("pool", W[:], wdv[:, :, nb_ * 128:(nb_ + 1) * 128], bW, writes=[bW])
                    for (t0, n, r) in tg:
                        ps, bps = getps()
                        for kb in range(44):
                            K.op("pe", lambda e, kb=kb, W=W, ps=ps: e.matmul(ps[:, :n], lhsT=W[:, kb, :], rhs=HID[:, kb, t0 - g0:t0 - g0 + n],
                                                                             start=(kb == 0), stop=(kb == 43)),
                                 reads=[bW, bHIDj[kb]], writes=[bps])
                        xi, bxi = XI[xctr[0] % 3]
                        xctr[0] += 1
                        K.dma("sp", xi[:, :n], XR[nb_ * 128:(nb_ + 1) * 128, t0:t0 + n], bxi, reads=[B_XR[nb_]], writes=[bxi])
                        K.op("dve", lambda e, xi=xi, ps=ps: e.scalar_tensor_tensor(
                            out=xi[:, :n], in0=ps[:, :n], scalar=GG2[:, r * 16 + nb_:r * 16 + nb_ + 1], in1=xi[:, :n],
                            op0=ALU.mult, op1=ALU.add), reads=[bps, bGG2, bxi], writes=[bxi])
                        K.dma("sp", XR[nb_ * 128:(nb_ + 1) * 128, t0:t0 + n], xi[:, :n], bxi, reads=[bxi], writes=[B_XR[nb_]])
                K.barrier()

    with ExitStack() as ph:
        XT, bXT = sb("XTf", [128, 16, 512], F32, ph)
        bXTk = [Buf("XTf%d" % k) for k in range(16)]
        SQ = [None, None]
        bSQ = [None, None]
        for i in range(2):
            SQ[i], bSQ[i] = sb("SQf%d" % i, [128, 512], BF16, ph)
        RS, bRS = sb("RSf", [128, 512], F32, ph)
        TM = [sb("TMf%d" % i, [128, 512], F32, ph) for i in range(2)]
        OT = [sb("OTf%d" % i, [128, 512], F32, ph) for i in range(2)]
        FG, bFG = sb("FG", [128, 16], F32, ph)
        K.dma("sp", FG[:], fng[:, :], bFG, writes=[bFG])
        for (c0, n, r) in TILES[1:]:
            for kb in range(16):
                K.dma("sp", XT[:, kb, :n], XR[kb * 128:(kb + 1) * 128, c0:c0 + n], bXTk[kb], reads=[B_XR[kb]], writes=[bXTk[kb]])
            ps, bps = sumsq([(XT[:, kb, :n], bXTk[kb]) for kb in range(16)], n, SQ, bSQ)
            rstd_from_psum(ps, bps, n, D, RS, bRS)
            for kb in range(16):
                tm, btm = TM[kb % 2]
                ot, bot = OT[kb % 2]
                K.op("dve", lambda e, kb=kb, tm=tm: e.tensor_tensor(out=tm[:, :n], in0=XT[:, kb, :n], in1=RS[:, :n], op=ALU.mult),
                     reads=[bXTk[kb], bRS], writes=[btm])
                K.op("act", lambda e, kb=kb, tm=tm, ot=ot: e.activation(out=ot[:, :n], in_=tm[:, :n], func=AF.Identity, scale=FG[:, kb:kb + 1]),
                     reads=[btm, bFG], writes=[bot])
                K.dma("sp", outT[kb * 128:(kb + 1) * 128, c0 - CTX:c0 - CTX + n], ot[:, :n], bot, reads=[bot], writes=[Buf()])
        if dbg:
            T0, bT0 = sb("T0d", [128, NT], F32, ph)
            T0b, bT0b = sb("T0bd", [128, NT], BF16, ph)
            for nm, ap_, shp, dt_ in (("XR", XR, [D, NT], F32), ("MIX", MIX, [D, NT], BF16), ("ATT", ATT, [1024, NT], F32),
                                      ("Us", Us, [512, NT], F32), ("QN", QN, [1024, NT], BF16), ("QR", QR, [512, NT], BF16),
                                      ("KRR", KRR, [64, NT], BF16), ("CQ", CQ, [512, NT], F32), ("KN", KN, [1024, NT], BF16)):
                o = nc.dram_tensor("dbg_" + nm, shp, dt_, kind="ExternalOutput").ap()
                Td, bTd = (T0, bT0) if dt_ == F32 else (T0b, bT0b)
                for r0 in range(0, shp[0], 128):
                    rr = min(128, shp[0] - r0)
                    K.dma("sp", Td[:rr, :], ap_[r0:r0 + rr, :], bTd, writes=[bTd])
                    K.dma("sp", o[r0:r0 + rr, :], Td[:rr, :], bTd, reads=[bTd], writes=[Buf()])
            o = nc.dram_tensor("dbg_Vs", [NT, 1024], BF16, kind="ExternalOutput").ap()
            for r0 in range(0, NT, 128):
                K.dma("sp", T0b[:, :1024], Vs[r0:r0 + 128, :], bT0b, writes=[bT0b])
                K.dma("sp", o[r0:r0 + 128, :], T0b[:, :1024], bT0b, reads=[bT0b], writes=[Buf()])
        K.barrier()
    es.close()
    return nc


def _pl(v, nb):
    return np.ascontiguousarray(v.reshape(nb, 128).T)


def _rope_perm():
    perm = np.zeros(64, np.int64)
    for r in range(64):
        axis, half, f = r // 32, (r % 32) // 16, r % 16
        perm[r] = axis * 32 + (1 - half) * 16 + f
    return perm


def _prep_shared(inp, nl):
    f32 = np.float32
    sh = {}
    sh["w_ada"] = np.ascontiguousarray(inp["w_ada"][:nl])
    sh["bada2"] = np.ascontiguousarray(np.stack([inp["b_ada"][:nl], inp["b_ada"][:nl]], axis=1))
    sh["n1g"] = np.stack([_pl(inp["norm1_g"][i], 16) for i in range(nl)])
    sh["n2g"] = np.stack([_pl(inp["norm2_g"][i], 16) for i in range(nl)])
    sh["mng"] = np.stack([_pl(inp["mix_norm"][i], 16) for i in range(nl)])
    sh["qng"] = np.stack([_pl(inp["mla_q_norm"][i], 4) for i in range(nl)])
    sh["kvng"] = np.stack([_pl(inp["mla_kv_norm"][i], 2) for i in range(nl)])
    sh["ssd"] = np.stack([_pl(inp["ssm_d"][i], 4) for i in range(nl)])
    sh["bglu"] = np.stack([_pl(inp["b_glu"][i], 4) for i in range(nl)])
    cw = inp["conv_w"][:nl]
    sh["convw"] = np.ascontiguousarray(cw.reshape(nl, 3, 4, 128).transpose(0, 3, 2, 1).reshape(nl, 128, 12))
    sh["fng"] = _pl(inp["final_norm"], 16)
    sh["w_in"] = np.ascontiguousarray(inp["w_in"][:nl])
    perm = _rope_perm()
    sh["w_krp"] = np.ascontiguousarray(inp["w_in"][:nl][:, :, 2304 + perm])
    sh["w_uq"] = np.ascontiguousarray(inp["w_uq"][:nl])
    cols = np.concatenate([192 * h + 128 + perm for h in range(8)])
    sh["w_uqp"] = np.ascontiguousarray(inp["w_uq"][:nl][:, :, cols])
    sh["w_ukv"] = np.ascontiguousarray(inp["w_ukv"][:nl])
    vcols = np.concatenate([256 * h + 128 + np.arange(128) for h in range(8)])
    sh["w_v"] = np.ascontiguousarray(inp["w_ukv"][:nl][:, :, vcols])
    t = np.arange(LAT)
    row = (t // 64).astype(f32)
    col = (t % 64).astype(f32)
    inv = (np.float32(10000.0) ** (-np.arange(16, dtype=f32) / np.float32(16))).astype(f32)
    rc = np.zeros((64, LAT), f32)
    rs = np.zeros((64, LAT), f32)
    for r in range(64):
        axis, half, f = r // 32, (r % 32) // 16, r % 16
        ang = ((row if axis == 0 else col) * inv[f]).astype(f32)
        rc[r] = np.cos(ang)
        rs[r] = np.sin(ang) * (-1.0 if half == 0 else 1.0)
    sh["ropec"] = rc
    sh["ropes"] = rs
    ssa = np.zeros((nl, 128, 2 * 3 * 16), f32)
    bw = np.zeros((nl, 128, 2, 2, 16, 128), f32)
    cwt = np.zeros((nl, 128, 2, 2, 16, 128), f32)
    for i in range(nl):
        for d in range(2):
            ssa[i, :, (d * 3 + 0) * 16:(d * 3 + 0) * 16 + 16] = _pl(inp["ssm_a_re"][i, d].reshape(-1), 16)
            ssa[i, :, (d * 3 + 1) * 16:(d * 3 + 1) * 16 + 16] = _pl(inp["ssm_a_im"][i, d].reshape(-1), 16)
            ssa[i, :, (d * 3 + 2) * 16:(d * 3 + 2) * 16 + 16] = _pl(np.repeat(inp["ssm_log_dt"][i, d], 64), 16)
            for ri, (bsrc, csrc) in enumerate(((inp["ssm_b_re"], inp["ssm_c_re"]), (inp["ssm_b_im"], inp["ssm_c_im"]))):
                for g in range(32):
                    rb, g2 = g // 2, g % 2
                    k0 = (g % 8) * 16
                    bw[i, k0:k0 + 16, d, ri, rb, g2 * 64:(g2 + 1) * 64] = bsrc[i, d, g].T
                    cwt[i, g2 * 64:(g2 + 1) * 64, d, ri, rb, k0:k0 + 16] = csrc[i, d, g].T
    sh["ssa"] = ssa
    sh["bw"] = bw.reshape(nl, 128, -1)
    sh["cw"] = cwt.reshape(nl, 128, -1)
    sh["w_glu"] = np.ascontiguousarray(inp["w_glu"][:nl])
    sh["w_o"] = np.ascontiguousarray(inp["w_o"][:nl])
    sh["w_gate"] = np.ascontiguousarray(inp["w_gate"][:nl])
    sh["w_up"] = np.ascontiguousarray(inp["w_up"][:nl])
    sh["w_down"] = np.ascontiguousarray(inp["w_down"][:nl])
    sh["iota"] = np.ascontiguousarray(np.broadcast_to(np.arange(256, dtype=f32), (128, 256)))
    sh["i2"] = np.eye(2, dtype=f32)
    return sh


def _prep_core(inp, b):
    xcat = np.concatenate([inp["ctx"][b], inp["x"][b]], axis=0)
    xT = np.ascontiguousarray(xcat.T)
    cc = np.stack([inp["c"][b], inp["c_ctx"]], axis=0)
    cs = np.ascontiguousarray(cc.reshape(2, 16, 128).transpose(2, 1, 0).reshape(128, 32))
    return {"xT": xT, "cs": cs}


N_CORES = 4


def kernel(**inputs):
    inp = {k: np.asarray(v) for k, v in inputs.items()}
    nl = inp["w_ada"].shape[0]
    nc = build(nl)
    shared = _prep_shared(inp, nl)
    in_maps = []
    for core in range(N_CORES):
        m = dict(shared)
        m.update(_prep_core(inp, core % 4))
        in_maps.append(m)
    res = run_bass_kernel_spmd(nc, in_maps, core_ids=list(range(N_CORES)))
    out = np.stack([np.ascontiguousarray(res.results[b]["outT"].T) for b in range(4)], axis=0)
    return out.astype(np.float32)
```

```python
import math
from contextlib import ExitStack

import numpy as np
import concourse.bass as bass
import concourse.mybir as mybir
from concourse.bass_utils import run_bass_kernel_spmd

F32, BF16 = mybir.dt.float32, mybir.dt.bfloat16
AF = mybir.ActivationFunctionType
ALU = mybir.AluOpType

D = 2048
LAT = 2048
CTX = 256
NT = LAT + CTX
DFF = 5632
DPROJ = 2880
EPS = 1e-6
MLA_SCALE = 192.0 ** -0.5
NL = 1152
TILES = [(0, 256, 1), (256, 512, 0), (768, 384, 0)]
GT = [(0, 0, 512), (0, 512, 512), (0, 1024, 128), (1, 0, 512), (1, 512, 512), (1, 1024, 128)]
GROWS = 392
PAIRS = [[0, 1], [2, 3], [4, 5], [6, 7]]
TWO_PI = 2.0 * math.pi


class Buf:
    def __init__(self, name=""):
        self.name = name
        self.w = None
        self.r = {}
        self.ds = None


class DSem:
    def __init__(self, sem):
        self.sem = sem
        self.tot = 0


class Ker:
    def __init__(self, nc, es):
        self.nc = nc
        self.es = es
        self.eng = {"pe": nc.tensor, "dve": nc.vector, "act": nc.scalar, "pool": nc.gpsimd, "sp": nc.sync}
        self.sem = {k: es.enter_context(nc.semaphore("sem_" + k)) for k in self.eng}
        self.cnt = {k: 0 for k in self.eng}
        self.waited = {k: {} for k in self.eng}
        self.dpool = [DSem(es.enter_context(nc.semaphore("dsem%d" % i))) for i in range(48)]
        self.dfree = list(self.dpool)
        self.dbufs = []
        self.cc = None

    def _wait(self, e, dep):
        sem, val = dep
        w = self.waited[e]
        if w.get(sem.num, 0) >= val:
            return
        self.eng[e].wait_ge(sem, val)
        w[sem.num] = val

    def _deps(self, e, reads, writes):
        deps = []
        mysem = self.sem[e]
        for b in reads:
            if b.w:
                deps.append(b.w)
        for b in writes:
            if b.w:
                deps.append(b.w)
            for k, d in b.r.items():
                if d[0] is mysem:
                    continue
                deps.append(d)
        for d in deps:
            if e == "pe" and d[0] is mysem:
                continue
            self._wait(e, d)

    def op(self, e, fn, reads=(), writes=()):
        self._deps(e, reads, writes)
        ins = fn(self.eng[e])
        self.cnt[e] += 1
        ins.then_inc(self.sem[e], 1)
        me = (self.sem[e], self.cnt[e])
        for b in writes:
            b.w = me
            b.r = {}
        for b in reads:
            if b not in writes:
                b.r[e] = me

    def dma(self, q, out, in_, sb, reads=(), writes=(), grouped=False, **kw):
        if sb.ds is None:
            sb.ds = self.dfree.pop()
            self.dbufs.append(sb)
        ds = sb.ds
        self._deps(q, reads, writes)
        if ds.tot > 0 and not grouped:
            self._wait(q, (ds.sem, ds.tot))
        self.eng[q].dma_start(out=out, in_=in_, **kw).then_inc(ds.sem, 16)
        ds.tot += 16
        me = (ds.sem, ds.tot)
        for b in writes:
            b.w = me
            b.r = {}
        for b in reads:
            if b not in writes:
                b.r["d%d" % ds.sem.num] = me

    def coll(self, ins, outs, reads, writes):
        if self.cc is None:
            self.cc = DSem(self.es.enter_context(self.nc.semaphore("ccsem")))
        self._deps("pool", reads, writes)
        if self.cc.tot > 0:
            self._wait("pool", (self.cc.sem, self.cc.tot))
        self.nc.gpsimd.collective_compute("AllGather", ALU.bypass, replica_groups=PAIRS, ins=ins, outs=outs).then_inc(self.cc.sem, 1)
        self.cc.tot += 1
        me = (self.cc.sem, self.cc.tot)
        for b in writes:
            b.w = me
            b.r = {}
        for b in reads:
            b.r["cc"] = me

    def barrier(self):
        alls = [(self.sem[k], self.cnt[k]) for k in self.eng] + [(d.sem, d.tot) for d in self.dpool]
        if self.cc is not None:
            alls.append((self.cc.sem, self.cc.tot))
        for e in self.eng:
            for d in alls:
                if d[1] > 0 and not (e == "pe" and d[0] is self.sem["pe"]):
                    self._wait(e, d)
        for b in self.dbufs:
            b.ds = None
        self.dbufs = []
        self.dfree = list(self.dpool)


def build(nl, dbg=False):
    nc = bass.Bass("TRN2", target_bir_lowering=False)
    es = ExitStack()
    K = Ker(nc, es)

    def din(name, shape, dt=F32):
        return nc.dram_tensor(name, list(shape), dt, kind="ExternalInput").ap()

    def dscr(name, shape, dt=F32):
        return nc.dram_tensor(name, list(shape), dt, kind="Internal").ap()

    xT = din("xT", [D, NL])
    cs_in = din("cs", [128, 32])
    w_ada = din("w_ada", [nl, D, 6 * D])
    bada2 = din("bada2", [nl, 2, 6 * D])
    n1g = din("n1g", [nl, 128, 16])
    n2g = din("n2g", [nl, 128, 16])
    mng = din("mng", [nl, 128, 16])
    qng = din("qng", [nl, 128, 4])
    kvng = din("kvng", [nl, 128, 2])
    ssd = din("ssd", [nl, 128, 4])
    bglu = din("bglu", [nl, 128, 4])
    convw = din("convw", [nl, 128, 12])
    fng = din("fng", [128, 16])
    w_in = din("w_in", [nl, D, DPROJ])
    w_krp = din("w_krp", [nl, D, 64])
    w_uq = din("w_uq", [nl, 512, 1536])
    w_uqp = din("w_uqp", [nl, 512, 512])
    w_ukv = din("w_ukv", [nl, 256, 2048])
    w_v = din("w_v", [nl, 256, 1024])
    ropqc = din("ropqc", [64, NL])
    ropqs = din("ropqs", [64, NL])
    ropkc = din("ropkc", [64, NT])
    ropks = din("ropks", [64, NT])
    maskl_in = din("maskl", [128, NL])
    maskr_in = din("maskr", [128, NL])
    sel_in = din("sel", [128, 8])
    kmask_in = din("kmask", [128, 18])
    ssa = din("ssa", [nl, 128, 2 * 3 * 16])
    bw_in = din("bw", [nl, 128, 2 * 2 * 16 * 128])
    cw_in = din("cw", [nl, 128, 2 * 2 * 16 * 128])
    w_glu = din("w_glu", [nl, 512, 512])
    w_o = din("w_o", [nl, D, D])
    w_gate = din("w_gate", [nl, D, DFF])
    w_up = din("w_up", [nl, D, DFF])
    w_down = din("w_down", [nl, DFF, D])
    iota_in = din("iota", [128, 256])
    i2_in = din("i2", [2, 2])
    outT = nc.dram_tensor("outT", [D, NL], F32, kind="ExternalOutput").ap()

    XR = dscr("XR", [D, NL])
    CQ = dscr("CQ", [512, NL])
    GIN = dscr("GIN", [GROWS, NL])
    GOUT = dscr("GOUT", [2 * GROWS, NL])
    XIN = dscr("XIN", [128, 32])
    XOUT = dscr("XOUT", [256, 32])
    Us = dscr("Us", [512, NL])
    MIX = dscr("MIX", [D, NL], BF16)
    QN = dscr("QN", [1024, NL], BF16)
    QR = dscr("QR", [512, NL], BF16)
    KN = dscr("KN", [1024, NT], BF16)
    KRR = dscr("KRR", [64, NT], BF16)
    Vs = dscr("Vs", [NT, 1024], BF16)
    ATT = dscr("ATT", [1024, NL])

    B_XR = [Buf("XR%d" % i) for i in range(16)]
    B_scr = {n: Buf(n) for n in ["CQ", "GIN", "GOUT", "XIN", "XOUT", "Us", "MIX", "QN", "QR", "KN", "KRR", "Vs", "ATT"]}

    sbc = [0]

    def sb(name, shape, dt=F32, stack=None):
        sbc[0] += 1
        t = (stack or es).enter_context(nc.sbuf_tensor("%s_%d" % (name, sbc[0]), list(shape), dt))
        return t, Buf(name)

    ONES, bONES = sb("ONES", [128, 128], BF16)
    CS, bCS = sb("CS", [128, 32], BF16)
    CSF, bCSF = sb("CSF", [128, 32])
    I2, bI2 = sb("I2", [2, 2])
    SEL, bSEL = sb("SEL", [128, 8])
    MODC, bMODC = sb("MODC", [128, 192])
    GS1, bGS1 = sb("GS1", [128, 32])
    SH1, bSH1 = sb("SH1", [128, 32])
    GG1, bGG1 = sb("GG1", [128, 32])
    GS2, bGS2 = sb("GS2", [128, 32])
    SH2, bSH2 = sb("SH2", [128, 32])
    GG2, bGG2 = sb("GG2", [128, 32])
    psum = []
    for i in range(8):
        t = es.enter_context(nc.psum_tensor("ps%d" % i, [128, 512], F32))
        psum.append((t, Buf("ps%d" % i)))
    pcnt = {}

    def getps(lo=0, hi=8):
        c = pcnt.get((lo, hi), 0)
        pcnt[(lo, hi)] = c + 1
        return psum[lo + c % (hi - lo)]

    K.op("dve", lambda e: e.memset(ONES[:], 1.0), writes=[bONES])
    K.dma("sp", CSF[:], cs_in[:, :], bCSF, writes=[bCSF])
    K.dma("sp", I2[:], i2_in[:, :], bI2, writes=[bI2])
    K.dma("sp", SEL[:], sel_in[:, :], bSEL, writes=[bSEL])
    K.op("act", lambda e: e.activation(out=CS[:], in_=CSF[:], func=AF.Silu), reads=[bCSF], writes=[bCS])
    with ExitStack() as ph:
        T0, bT0 = sb("T0", [128, NL], F32, ph)
        T1, bT1 = sb("T1", [128, NL], F32, ph)
        for kb in range(16):
            T, bT = (T0, bT0) if kb % 2 == 0 else (T1, bT1)
            K.dma("sp", T[:], xT[kb * 128:(kb + 1) * 128, :], bT, writes=[bT])
            K.dma("sp", XR[kb * 128:(kb + 1) * 128, :], T[:], bT, reads=[bT], writes=[B_XR[kb]])
        K.barrier()

    def rstd_from_psum(ps, bps, n, dim, RS, bRS):
        K.op("dve", lambda e: e.tensor_scalar(out=RS[:, :n], in0=ps[:, :n], scalar1=1.0 / dim, scalar2=EPS,
                                              op0=ALU.mult, op1=ALU.add), reads=[bps], writes=[bRS])
        K.op("act", lambda e: e.activation(out=RS[:, :n], in_=RS[:, :n], func=AF.Sqrt), reads=[bRS], writes=[bRS])
        K.op("dve", lambda e: e.reciprocal(out=RS[:, :n], in_=RS[:, :n]), reads=[bRS], writes=[bRS])

    def sumsq(blocks, n, SQ, bSQ):
        ps, bps = getps()
        nb = len(blocks)
        for i, (ap, bb) in enumerate(blocks):
            sq, bsq = SQ[i % 2], bSQ[i % 2]
            K.op("act", lambda e, ap=ap, sq=sq: e.activation(out=sq[:, :n], in_=ap, func=AF.Square),
                 reads=[bb], writes=[bsq])
            K.op("pe", lambda e, sq=sq, i=i: e.matmul(ps[:, :n], lhsT=ONES[:], rhs=sq[:, :n],
                                                       start=(i == 0), stop=(i == nb - 1)),
                 reads=[bsq, bONES], writes=[bps])
        return ps, bps

    def wview(wap, k):
        return wap.rearrange("(kb p) n -> p kb n", p=128)

    def norm_scratch(ph, tag):
        SQ = [None, None]
        bSQ = [None, None]
        for i in range(2):
            SQ[i], bSQ[i] = sb("SQ%s%d" % (tag, i), [128, 512], BF16, ph)
        RS, bRS = sb("RS" + tag, [128, 512], F32, ph)
        TM = [sb("TM%s%d" % (tag, i), [128, 512], F32, ph) for i in range(2)]
        return SQ, bSQ, RS, bRS, TM

    def norm_mod(tiles, HT, bHT, GS, bGS, SH, bSH, ph):
        XT, bXT = sb("XTn", [128, 16, 512], F32, ph)
        bXTk = [Buf("XTk%d" % k) for k in range(16)]
        SQ, bSQ, RS, bRS, TM = norm_scratch(ph, "n")
        for (c0, n, r) in tiles:
            for kb in range(16):
                K.dma("sp", XT[:, kb, :n], XR[kb * 128:(kb + 1) * 128, c0:c0 + n], bXTk[kb],
                      reads=[B_XR[kb]], writes=[bXTk[kb]])
            ps, bps = sumsq([(XT[:, kb, :n], bXTk[kb]) for kb in range(16)], n, SQ, bSQ)
            rstd_from_psum(ps, bps, n, D, RS, bRS)
            for kb in range(16):
                tm, btm = TM[kb % 2]
                K.op("dve", lambda e, kb=kb, tm=tm: e.tensor_tensor(out=tm[:, :n], in0=XT[:, kb, :n], in1=RS[:, :n],
                                                                   op=ALU.mult),
                     reads=[bXTk[kb], bRS], writes=[btm])
                K.op("act", lambda e, kb=kb, tm=tm: e.activation(
                    out=HT[:, kb, c0:c0 + n], in_=tm[:, :n], func=AF.Identity,
                    bias=SH[:, r * 16 + kb:r * 16 + kb + 1], scale=GS[:, r * 16 + kb:r * 16 + kb + 1]),
                    reads=[btm, bSH, bGS], writes=[bHT])

    def proj(groups, kblocks, rhs_fn, rhs_bufs, tiles, consumer, wpool):
        for gi, segs in enumerate(groups):
            wts = []
            for si, (wv, c0, M) in enumerate(segs):
                W, bW = wpool[(gi % 2) * 3 + si]
                K.dma("pool", W[:, :kblocks, :M], wv[:, :, c0:c0 + M], bW, writes=[bW])
                wts.append((W, bW, M))
            for tile in tiles:
                (t0, n) = tile[0], tile[1]
                outs = []
                for (W, bW, M) in wts:
                    ps, bps = getps()
                    for kb in range(kblocks):
                        K.op("pe", lambda e, kb=kb, W=W, M=M, ps=ps: e.matmul(
                            ps[:M, :n], lhsT=W[:, kb, :M], rhs=rhs_fn(kb, t0, n),
                            start=(kb == 0), stop=(kb == kblocks - 1)),
                            reads=[bW] + rhs_bufs, writes=[bps])
                    outs.append((ps, bps, M))
                consumer(gi, tile, outs)

    def merge_norm(blocks_fn, dim, gain_col0, mix_row0, tiles, ph, tag, MN, bMN, STB):
        SQ, bSQ, RS, bRS, TM = norm_scratch(ph, tag)
        for (t0, n, r) in tiles:
            blocks = blocks_fn(t0, n)
            ps, bps = sumsq(blocks, n, SQ, bSQ)
            rstd_from_psum(ps, bps, n, dim, RS, bRS)
            for j, (ap, bb) in enumerate(blocks):
                tm, btm = TM[j % 2]
                K.op("dve", lambda e, ap=ap, tm=tm: e.tensor_tensor(out=tm[:, :n], in0=ap, in1=RS[:, :n], op=ALU.mult),
                     reads=[bb, bRS], writes=[btm])
                S, bS = STB[j % 3]
                K.op("act", lambda e, tm=tm, S=S, j=j: e.activation(
                    out=S[:, :n], in_=tm[:, :n], func=AF.Identity,
                    scale=MN[:, gain_col0 + j:gain_col0 + j + 1]), reads=[btm, bMN], writes=[bS])
                K.dma("sp", MIX[mix_row0 + j * 128:mix_row0 + (j + 1) * 128, t0:t0 + n], S[:, :n], bS,
                      reads=[bS], writes=[B_scr["MIX"]])

    for li in range(nl):
        with ExitStack() as ph:
            MODROW, bMODROW = sb("MODROW", [2, 6 * D], F32, ph)
            BROW, bBROW = sb("BROW", [2, 6 * D], F32, ph)
            WA = [sb("WA%d" % i, [128, 16, 512], BF16, ph) for i in range(2)]
            NG1, bNG1 = sb("NG1", [128, 16], F32, ph)
            NG2, bNG2 = sb("NG2", [128, 16], F32, ph)
            K.dma("sp", BROW[:], bada2[li], bBROW, writes=[bBROW])
            K.dma("sp", NG1[:], n1g[li], bNG1, writes=[bNG1])
            K.dma("sp", NG2[:], n2g[li], bNG2, writes=[bNG2])
            wav = wview(w_ada[li], 16)
            for ch in range(24):
                W, bW = WA[ch % 2]
                K.dma("pool", W[:], wav[:, :, ch * 512:(ch + 1) * 512], bW, writes=[bW])
                ps, bps = getps()
                for kb in range(16):
                    K.op("pe", lambda e, kb=kb, W=W: e.matmul(ps[0:2, :], lhsT=CS[:, kb * 2:kb * 2 + 2], rhs=W[:, kb, :],
                                                              start=(kb == 0), stop=(kb == 15)),
                         reads=[bW, bCS], writes=[bps])
                K.op("dve", lambda e, ch=ch: e.tensor_tensor(out=MODROW[:, ch * 512:(ch + 1) * 512], in0=ps[0:2, :],
                                                             in1=BROW[:, ch * 512:(ch + 1) * 512], op=ALU.add),
                     reads=[bps, bBROW], writes=[bMODROW])
            ps, bps = getps()
            for blk in range(96):
                K.op("pe", lambda e, blk=blk: e.matmul(ps[:, blk * 2:blk * 2 + 2], lhsT=MODROW[0:2, blk * 128:(blk + 1) * 128],
                                                       rhs=I2[0:2, 0:2], start=True, stop=True),
                     reads=[bMODROW, bI2], writes=[bps])
            K.op("dve", lambda e: e.tensor_copy(out=MODC[:], in_=ps[:, 0:192]), reads=[bps], writes=[bMODC])
            mv = MODC[:].rearrange("p (j kb r) -> p j kb r", j=6, kb=16, r=2)
            for r in range(2):
                sl = slice(r * 16, (r + 1) * 16)
                for (dst, bdst, jsc, jsh, jg, SH, bSH, GG, bGG, NG, bNG) in (
                        (GS1, bGS1, 1, 0, 2, SH1, bSH1, GG1, bGG1, NG1, bNG1),
                        (GS2, bGS2, 4, 3, 5, SH2, bSH2, GG2, bGG2, NG2, bNG2)):
                    K.op("dve", lambda e, dst=dst, jsc=jsc, NG=NG: e.scalar_tensor_tensor(
                        out=dst[:, sl], in0=mv[:, jsc, :, r], scalar=1.0, in1=NG[:], op0=ALU.add, op1=ALU.mult),
                        reads=[bMODC, bNG], writes=[bdst])
                    K.op("dve", lambda e, SH=SH, jsh=jsh: e.tensor_copy(out=SH[:, sl], in_=mv[:, jsh, :, r]),
                         reads=[bMODC], writes=[bSH])
                    K.op("dve", lambda e, GG=GG, jg=jg: e.tensor_copy(out=GG[:, sl], in_=mv[:, jg, :, r]),
                         reads=[bMODC], writes=[bGG])
            K.barrier()

        stg_ctr = [0]

        with ExitStack() as ph:
            HT, bHT = sb("HT", [128, 16, NL], BF16, ph)
            with ExitStack() as ph2:
                norm_mod(TILES, HT, bHT, GS1, bGS1, SH1, bSH1, ph2)
                K.barrier()
            WP = [sb("WP%d" % i, [128, 16, 128], BF16, ph) for i in range(6)]
            STG = [sb("STG%d" % i, [128, 512], F32, ph) for i in range(4)]
            STB = [sb("STB%d" % i, [128, 512], BF16, ph) for i in range(3)]
            CH, bCH = sb("CH", [128, 512], F32, ph)
            CT, bCT_ = sb("CT", [128, 4, NL + 2], F32, ph)
            bCTj = [Buf("CT%d" % j) for j in range(4)]
            CB, bCB_ = sb("CB", [128, 4, NL], F32, ph)
            bCBj = [Buf("CB%d" % j) for j in range(4)]
            CY, bCY = sb("CY", [128, NL], F32, ph)
            CZ, bCZ = sb("CZ", [128, NL], F32, ph)
            YC, bYC = sb("YC", [128, 4, NL], F32, ph)
            bYCj = [Buf("YC%d" % j) for j in range(4)]
            CW_, bCW_ = sb("CWc", [128, 12], F32, ph)
            MN, bMN = sb("MN", [128, 16], F32, ph)
            MKL, bMKL = sb("MKL", [128, NL], F32, ph)
            MKR, bMKR = sb("MKR", [128, NL], F32, ph)
            HL0, bHL0 = sb("HL0", [128, 4], F32, ph)
            HL1, bHL1 = sb("HL1", [128, 4], F32, ph)
            K.dma("sp", CW_[:], convw[li], bCW_, writes=[bCW_])
            K.dma("sp", MN[:], mng[li], bMN, writes=[bMN])
            K.dma("sp", MKL[:], maskl_in[:, :], bMKL, writes=[bMKL])
            K.dma("sp", MKR[:], maskr_in[:, :], bMKR, writes=[bMKR])
            K.op("dve", lambda e: e.memset(CT[:], 0.0), writes=bCTj)
            wiv = wview(w_in[li], 16)
            wkv = wview(w_krp[li], 16)
            groups = []
            kinds = []
            for j in range(4):
                groups.append([(wiv, j * 128, 128), (wiv, 1024 + j * 128, 128), (wiv, 512 + j * 128, 128)])
                kinds.append(("conv", j))
            for j in range(2):
                groups.append([(wiv, 2048 + j * 128, 128)])
                kinds.append(("ckv", j))
            groups.append([(wiv, 2304, 64), (wkv, 0, 64)])
            kinds.append(("kr", 0))
            for j in range(4):
                groups.append([(wiv, 1536 + j * 128, 128)])
                kinds.append(("cq", j))
            for j in range(4):
                groups.append([(wiv, 2368 + j * 128, 128)])
                kinds.append(("u", j))

            def store_f32(ps, bps, M, n, dram_ap, dbuf):
                i = stg_ctr[0] % 4
                stg_ctr[0] += 1
                S, bS = STG[i]
                if i % 2 == 0:
                    K.op("act", lambda e: e.copy(out=S[:M, :n], in_=ps[:M, :n]), reads=[bps], writes=[bS])
                else:
                    K.op("dve", lambda e: e.tensor_copy(out=S[:M, :n], in_=ps[:M, :n]), reads=[bps], writes=[bS])
                K.dma("sp", dram_ap, S[:M, :n], bS, reads=[bS], writes=[dbuf])

            def consumer(gi, tile, outs):
                (t0, n, r) = tile
                kind, j = kinds[gi]
                if kind == "conv":
                    (p0, b0, _), (p1, b1, _), (p2, b2, _) = outs
                    K.op("act", lambda e: e.copy(out=CH[:, :n], in_=p0[:, :n]), reads=[b0], writes=[bCH])
                    K.op("dve", lambda e: e.tensor_tensor(out=CT[:, j, 1 + t0:1 + t0 + n], in0=p1[:, :n], in1=CH[:, :n],
                                                          op=ALU.mult), reads=[b1, bCH], writes=[bCTj[j]])
                    K.op("act", lambda e: e.copy(out=CB[:, j, t0:t0 + n], in_=p2[:, :n]), reads=[b2], writes=[bCBj[j]])
                elif kind == "cq":
                    store_f32(outs[0][0], outs[0][1], 128, n, CQ[j * 128:(j + 1) * 128, t0:t0 + n], B_scr["CQ"])
                elif kind == "ckv":
                    store_f32(outs[0][0], outs[0][1], 128, n, GIN[j * 128:(j + 1) * 128, t0:t0 + n], B_scr["GIN"])
                elif kind == "kr":
                    store_f32(outs[0][0], outs[0][1], 64, n, GIN[256:320, t0:t0 + n], B_scr["GIN"])
                    store_f32(outs[1][0], outs[1][1], 64, n, GIN[320:384, t0:t0 + n], B_scr["GIN"])
                elif kind == "u":
                    store_f32(outs[0][0], outs[0][1], 128, n, Us[j * 128:(j + 1) * 128, t0:t0 + n], B_scr["Us"])
                if kind == "kr" and t0 == TILES[-1][0]:
                    for jj in range(4):
                        K.dma("sp", GIN[384 + jj:385 + jj, 0:128].rearrange("o (p x) -> (o p) x", x=1), CT[:, jj, NL:NL + 1],
                              bCTj[jj], reads=[bCTj[jj]], writes=[B_scr["GIN"]])
                    K.coll([GIN], [GOUT], [B_scr["GIN"]], [B_scr["GOUT"]])

            proj(groups, 16, lambda kb, t0, n: HT[:, kb, t0:t0 + n], [bHT], TILES, consumer, WP)

            for jj in range(4):
                K.dma("sp", HL0[:, jj:jj + 1], GOUT[384 + jj:385 + jj, 0:128].rearrange("o (p x) -> (o p) x", x=1), bHL0,
                      reads=[B_scr["GOUT"]], writes=[bHL0], grouped=(jj > 0))
                K.dma("sp", HL1[:, jj:jj + 1], GOUT[GROWS + 384 + jj:GROWS + 385 + jj, 0:128].rearrange("o (p x) -> (o p) x", x=1), bHL1,
                      reads=[B_scr["GOUT"]], writes=[bHL1], grouped=(jj > 0))
            K.op("dve", lambda e: e.tensor_scalar(out=HL0[:], in0=HL0[:], scalar1=SEL[:, 0:1], scalar2=None, op0=ALU.mult),
                 reads=[bHL0, bSEL], writes=[bHL0])
            K.op("dve", lambda e: e.scalar_tensor_tensor(out=HL0[:], in0=HL1[:], scalar=SEL[:, 1:2], in1=HL0[:], op0=ALU.mult, op1=ALU.add),
                 reads=[bHL0, bHL1, bSEL], writes=[bHL0])
            for j in range(4):
                K.op("dve", lambda e, j=j: e.tensor_copy(out=CT[:, j, NL + 1:NL + 2], in_=HL0[:, j:j + 1]), reads=[bHL0], writes=[bCTj[j]])
                K.op("dve", lambda e, j=j: e.tensor_scalar(out=CY[:], in0=CT[:, j, 1:NL + 1], scalar1=CW_[:, j * 3 + 1:j * 3 + 2], scalar2=None,
                                                           op0=ALU.mult), reads=[bCTj[j], bCW_], writes=[bCY])
                K.op("pool", lambda e, j=j: e.tensor_tensor(out=CZ[:], in0=CT[:, j, 0:NL], in1=MKL[:], op=ALU.mult),
                     reads=[bCTj[j], bMKL], writes=[bCZ])
                K.op("dve", lambda e, j=j: e.scalar_tensor_tensor(out=CY[:], in0=CZ[:], scalar=CW_[:, j * 3:j * 3 + 1], in1=CY[:],
                                                                  op0=ALU.mult, op1=ALU.add), reads=[bCZ, bCW_, bCY], writes=[bCY])
                K.op("pool", lambda e, j=j: e.tensor_tensor(out=CZ[:], in0=CT[:, j, 2:NL + 2], in1=MKR[:], op=ALU.mult),
                     reads=[bCTj[j], bMKR], writes=[bCZ])
                K.op("dve", lambda e, j=j: e.scalar_tensor_tensor(out=CY[:], in0=CZ[:], scalar=CW_[:, j * 3 + 2:j * 3 + 3], in1=CY[:],
                                                                  op0=ALU.mult, op1=ALU.add), reads=[bCZ, bCW_, bCY], writes=[bCY])
                K.op("dve", lambda e, j=j: e.tensor_tensor(out=YC[:, j, :], in0=CY[:], in1=CB[:, j, :], op=ALU.mult),
                     reads=[bCY, bCBj[j]], writes=[bYCj[j]])
            merge_norm(lambda t0, n: [(YC[:, j, t0:t0 + n], bYCj[j]) for j in range(4)], 512, 0, 0, TILES, ph, "m", MN, bMN, STB)
            K.barrier()

        with ExitStack() as ph:
            CQN, bCQN = sb("CQN", [128, 4, NL], BF16, ph)
            CKN, bCKN = sb("CKN", [128, 2, NT], BF16, ph)
            XB, bXB_ = sb("XBq", [128, 4, 512], F32, ph)
            bXB = [Buf("XBq%d" % k) for k in range(4)]
            SQ, bSQ, RS, bRS, TM = norm_scratch(ph, "q")
            QG, bQG = sb("QG", [128, 4], F32, ph)
            KG, bKG = sb("KG", [128, 2], F32, ph)
            RQC, bRQC = sb("RQC", [64, NL], F32, ph)
            RQS, bRQS = sb("RQS", [64, NL], F32, ph)
            RKC, bRKC = sb("RKC", [64, NT], F32, ph)
            RKS, bRKS = sb("RKS", [64, NT], F32, ph)
            WP = [sb("WQ%d" % i, [128, 4, 128], BF16, ph) for i in range(6)]
            WV, bWV = sb("WV", [128, 2, 1024], BF16, ph)
            STB = [sb("STBq%d" % i, [128, 512], BF16, ph) for i in range(4)]
            M1 = [sb("M1q%d" % i, [64, 512], F32, ph) for i in range(2)]
            M2 = [sb("M2q%d" % i, [64, 512], F32, ph) for i in range(2)]
            KX, bKX = sb("KX", [64, 512], F32, ph)
            KXP, bKXP = sb("KXP", [64, 512], F32, ph)
            K.dma("sp", QG[:], qng[li], bQG, writes=[bQG])
            K.dma("sp", KG[:], kvng[li], bKG, writes=[bKG])
            K.dma("sp", RQC[:], ropqc[:, :], bRQC, writes=[bRQC])
            K.dma("sp", RQS[:], ropqs[:, :], bRQS, writes=[bRQS])
            K.dma("sp", RKC[:], ropkc[:, :], bRKC, writes=[bRKC])
            K.dma("sp", RKS[:], ropks[:, :], bRKS, writes=[bRKS])
            K.dma("pool", WV[:], wview(w_v[li], 2), bWV, writes=[bWV])

            def small_norm(src_fn, srcbuf, nb, dim, G, bG, DST, bDST, tiles):
                for (d0, n, srcf) in tiles:
                    for k in range(nb):
                        K.dma("sp", XB[:, k, :n], srcf(k), bXB[k], reads=[srcbuf], writes=[bXB[k]])
                    ps, bps = sumsq([(XB[:, k, :n], bXB[k]) for k in range(nb)], n, SQ, bSQ)
                    rstd_from_psum(ps, bps, n, dim, RS, bRS)
                    for k in range(nb):
                        tm, btm = TM[k % 2]
                        K.op("dve", lambda e, k=k, tm=tm: e.tensor_tensor(out=tm[:, :n], in0=XB[:, k, :n], in1=RS[:, :n],
                                                                         op=ALU.mult), reads=[bXB[k], bRS], writes=[btm])
                        K.op("act", lambda e, k=k, tm=tm: e.activation(out=DST[:, k, d0:d0 + n], in_=tm[:, :n], func=AF.Identity,
                                                                      scale=G[:, k:k + 1]), reads=[btm, bG], writes=[bDST])

            small_norm(None, B_scr["CQ"], 4, 512, QG, bQG, CQN, bCQN,
                       [(t0, n, (lambda k, t0=t0, n=n: CQ[k * 128:(k + 1) * 128, t0:t0 + n])) for (t0, n, r) in TILES])
            small_norm(None, B_scr["GOUT"], 2, 256, KG, bKG, CKN, bCKN,
                       [(rk * NL + c0, n, (lambda k, rk=rk, c0=c0, n=n: GOUT[rk * GROWS + k * 128:rk * GROWS + (k + 1) * 128, c0:c0 + n]))
                        for (rk, c0, n) in GT])

            sctr = [0]

            def store_bf(ps, bps, M, n, dram_ap, dbuf):
                i = sctr[0] % 4
                sctr[0] += 1
                S, bS = STB[i]
                if i % 2 == 0:
                    K.op("act", lambda e: e.copy(out=S[:M, :n], in_=ps[:M, :n]), reads=[bps], writes=[bS])
                else:
                    K.op("dve", lambda e: e.tensor_copy(out=S[:M, :n], in_=ps[:M, :n]), reads=[bps], writes=[bS])
                K.dma("sp", dram_ap, S[:M, :n], bS, reads=[bS], writes=[dbuf])

            def rope_store(x, bx, xp, bxp, n, tc_ap, bc, ts_ap, bs_, dram_ap, dbuf):
                i = sctr[0] % 4
                sctr[0] += 1
                S, bS = STB[i]
                m1, bm1 = M1[i % 2]
                m2, bm2 = M2[i % 2]
                K.op("dve", lambda e: e.tensor_tensor(out=m1[:, :n], in0=x, in1=tc_ap, op=ALU.mult), reads=[bx, bc], writes=[bm1])
                K.op("dve", lambda e: e.tensor_tensor(out=m2[:, :n], in0=xp, in1=ts_ap, op=ALU.mult), reads=[bxp, bs_], writes=[bm2])
                K.op("pool", lambda e: e.tensor_tensor(out=S[:64, :n], in0=m1[:, :n], in1=m2[:, :n], op=ALU.add),
                     reads=[bm1, bm2], writes=[bS])
                K.dma("sp", dram_ap, S[:64, :n], bS, reads=[bS], writes=[dbuf])

            wuv = wview(w_uq[li], 4)
            wupv = wview(w_uqp[li], 4)
            groups = [[(wuv, 192 * h, 128), (wuv, 192 * h + 128, 64), (wupv, 64 * h, 64)] for h in range(8)]

            def cons_q(gi, tile, outs):
                (t0, n, r) = tile
                h = gi
                store_bf(outs[0][0], outs[0][1], 128, n, QN[h * 128:(h + 1) * 128, t0:t0 + n], B_scr["QN"])
                rope_store(outs[1][0][:64, :n], outs[1][1], outs[2][0][:64, :n], outs[2][1], n,
                           RQC[:, t0:t0 + n], bRQC, RQS[:, t0:t0 + n], bRQS,
                           QR[h * 64:(h + 1) * 64, t0:t0 + n], B_scr["QR"])

            proj(groups, 4, lambda kb, t0, n: CQN[:, kb, t0:t0 + n], [bCQN], TILES, cons_q, WP)

            gtiles = [(rk * NL + c0, n) for (rk, c0, n) in GT]
            wkvv = wview(w_ukv[li], 2)
            groups = [[(wkvv, 256 * h, 128)] for h in range(8)]

            def cons_k(gi, tile, outs):
                (t0, n) = tile
                store_bf(outs[0][0], outs[0][1], 128, n, KN[gi * 128:(gi + 1) * 128, t0:t0 + n], B_scr["KN"])

            proj(groups, 2, lambda kb, t0, n: CKN[:, kb, t0:t0 + n], [bCKN], gtiles, cons_k, WP)

            for tt in range(NT // 128):
                for hf in range(2):
                    ps, bps = getps()
                    for kb in range(2):
                        K.op("pe", lambda e, kb=kb, ps=ps: e.matmul(ps[:, :], lhsT=CKN[:, kb, tt * 128:(tt + 1) * 128],
                                                                    rhs=WV[:, kb, hf * 512:(hf + 1) * 512],
                                                                    start=(kb == 0), stop=(kb == 1)),
                             reads=[bCKN, bWV], writes=[bps])
                    store_bf(ps, bps, 128, 512, Vs[tt * 128:(tt + 1) * 128, hf * 512:(hf + 1) * 512], B_scr["Vs"])

            for (rk, c0, n) in GT:
                g0 = rk * NL + c0
                K.dma("sp", KX[:, :n], GOUT[rk * GROWS + 256:rk * GROWS + 320, c0:c0 + n], bKX, reads=[B_scr["GOUT"]], writes=[bKX])
                K.dma("sp", KXP[:, :n], GOUT[rk * GROWS + 320:rk * GROWS + 384, c0:c0 + n], bKXP, reads=[B_scr["GOUT"]], writes=[bKXP])
                rope_store(KX[:, :n], bKX, KXP[:, :n], bKXP, n, RKC[:, g0:g0 + n], bRKC, RKS[:, g0:g0 + n], bRKS,
                           KRR[:, g0:g0 + n], B_scr["KRR"])
            K.barrier()

        with ExitStack() as ph:
            KNh = [sb("KNh%d" % i, [128, NT], BF16, ph) for i in range(2)]
            QNh = [sb("QNh%d" % i, [128, NL], BF16, ph) for i in range(2)]
            QRh = [sb("QRh%d" % i, [64, NL], BF16, ph) for i in range(2)]
            Vh = [sb("Vh%d" % i, [128, 18, 128], BF16, ph) for i in range(2)]
            KRh, bKRh = sb("KRh", [64, NT], BF16, ph)
            KM, bKM = sb("KM", [128, 18], F32, ph)
            PT = [sb("PT%d" % i, [128, 512], BF16, ph) for i in range(3)]
            RD, bRD = sb("RD", [128, 512], F32, ph)
            OS = [sb("OS%d" % i, [128, 512], F32, ph) for i in range(2)]
            K.dma("sp", KRh[:], KRR[:, :], bKRh, reads=[B_scr["KRR"]], writes=[bKRh])
            K.dma("sp", KM[:], kmask_in[:, :], bKM, writes=[bKM])
            vview = Vs.rearrange("(kb p) d -> p kb d", p=128)
            octr = 0
            nkb = 18
            for h in range(8):
                kn, bkn = KNh[h % 2]
                qn, bqn = QNh[h % 2]
                qr, bqr = QRh[h % 2]
                vh, bvh = Vh[h % 2]
                K.dma("sp", kn[:], KN[h * 128:(h + 1) * 128, :], bkn, reads=[B_scr["KN"]], writes=[bkn])
                K.dma("sp", qn[:], QN[h * 128:(h + 1) * 128, :], bqn, reads=[B_scr["QN"]], writes=[bqn])
                K.dma("sp", qr[:], QR[h * 64:(h + 1) * 64, :], bqr, reads=[B_scr["QR"]], writes=[bqr])
                K.dma("sp", vh[:], vview[:, :, h * 128:(h + 1) * 128], bvh, reads=[B_scr["Vs"]], writes=[bvh])
                for (t0, n, r) in TILES:
                    po, bpo = getps(0, 4)
                    pd, bpd = getps(0, 4)
                    pend = None
                    for kb in range(nkb + 1):
                        cur = None
                        if kb < nkb:
                            ps_, bps_ = getps(4, 8)
                            ks = slice(kb * 128, (kb + 1) * 128)
                            K.op("pe", lambda e, ps_=ps_, ks=ks: e.matmul(ps_[:, :n], lhsT=kn[:, ks], rhs=qn[:, t0:t0 + n],
                                                                          start=True, stop=False),
                                 reads=[bkn, bqn], writes=[bps_])
                            K.op("pe", lambda e, ps_=ps_, ks=ks: e.matmul(ps_[:, :n], lhsT=KRh[:, ks], rhs=qr[:, t0:t0 + n],
                                                                          start=False, stop=True),
                                 reads=[bKRh, bqr], writes=[bps_])
                            pt, bpt = PT[kb % 3]
                            if r == 1:
                                K.op("act", lambda e, ps_=ps_, pt=pt, kb=kb: e.activation(out=pt[:, :n], in_=ps_[:, :n], func=AF.Exp,
                                                                                          bias=KM[:, kb:kb + 1], scale=MLA_SCALE),
                                     reads=[bps_, bKM], writes=[bpt])
                            else:
                                K.op("act", lambda e, ps_=ps_, pt=pt: e.activation(out=pt[:, :n], in_=ps_[:, :n], func=AF.Exp,
                                                                                   scale=MLA_SCALE),
                                     reads=[bps_], writes=[bpt])
                            cur = (kb, pt, bpt)
                        if pend is not None:
                            pk, ppt, pbpt = pend
                            K.op("pe", lambda e, pk=pk, ppt=ppt: e.matmul(po[:, :n], lhsT=vh[:, pk, :], rhs=ppt[:, :n],
                                                                          start=(pk == 0), stop=(pk == nkb - 1)),
                                 reads=[bvh, pbpt], writes=[bpo])
                            K.op("pe", lambda e, pk=pk, ppt=ppt: e.matmul(pd[:, :n], lhsT=ONES[:], rhs=ppt[:, :n],
                                                                          start=(pk == 0), stop=(pk == nkb - 1)),
                                 reads=[bONES, pbpt], writes=[bpd])
                        pend = cur
                    K.op("dve", lambda e: e.reciprocal(out=RD[:, :n], in_=pd[:, :n]), reads=[bpd], writes=[bRD])
                    os_, bos = OS[octr % 2]
                    octr += 1
                    K.op("dve", lambda e, os_=os_: e.tensor_tensor(out=os_[:, :n], in0=po[:, :n], in1=RD[:, :n], op=ALU.mult),
                         reads=[bpo, bRD], writes=[bos])
                    K.dma("sp", ATT[h * 128:(h + 1) * 128, t0:t0 + n], os_[:, :n], bos, reads=[bos], writes=[B_scr["ATT"]])
            K.barrier()

        with ExitStack() as ph:
            UT, bUT = sb("UT", [128, 4, NL], BF16, ph)
            YACC, bYACC_ = sb("YACC", [128, 4, NL], F32, ph)
            bYA = [Buf("YA%d" % c) for c in range(4)]
            ph2 = ExitStack()
            _ph_outer = ph
            ph = ph2
            BW, bBW = sb("BWs", [128, 2 * 2 * 16 * 128], BF16, ph)
            CWf, bCWf = sb("CWf", [128, 2 * 16 * 128], F32, ph)
            CWb, bCWb = sb("CWb", [128, 2 * 16 * 128], BF16, ph)
            SA, bSA = sb("SA", [128, 96], F32, ph)
            IOT, bIOT = sb("IOT", [128, 256], F32, ph)
            TABC, bTABC = sb("TABC", [128, 16, 256], F32, ph)
            TABS, bTABS = sb("TABS", [128, 16, 256], F32, ph)
            RHOT, bRHOT = sb("RHOT", [128, 16, 256], F32, ph)
            ANG, bANG = sb("ANG", [128, 256], F32, ph)
            ANG2, bANG2 = sb("ANG2", [128, 256], F32, ph)
            ANG3, bANG3 = sb("ANG3", [128, 256], F32, ph)
            XG0, bXG0 = sb("XG0", [128, 32], F32, ph)
            XG1, bXG1 = sb("XG1", [128, 32], F32, ph)
            XS, bXS = sb("XS", [128, 32], F32, ph)
            sm = {}
            for nm in ["DT", "LR", "LI", "RHO", "ER", "EI", "NEI", "ER2", "EI2", "NEI2", "T1", "T2", "T3", "DEN", "KR_", "KI_", "NKR", "NKI",
                       "CAR", "CAI", "ABR", "ABI", "TL1", "TL2", "AE", "AE2", "AE3"]:
                sm[nm] = sb("s5_" + nm, [128, 16], F32, ph)
            mt = {}
            for nm in ["m1", "m2", "m3", "m4", "gir", "gii", "GR", "GI", "m5", "m6", "m7", "m8"]:
                mt[nm] = sb("s5t_" + nm, [128, 256], F32, ph)
            HRt = [sb("HRt%d" % i, [128, 256], BF16, ph) for i in range(2)]
            HIt = [sb("HIt%d" % i, [128, 256], BF16, ph) for i in range(2)]
            XU, bXU = sb("XU", [128, 512], F32, ph)
            K.dma("sp", IOT[:], iota_in[:, :], bIOT, writes=[bIOT])
            K.dma("sp", SA[:], ssa[li], bSA, writes=[bSA])
            K.dma("pool", BW[:], bw_in[li], bBW, writes=[bBW])
            for cb in range(4):
                K.op("dve", lambda e, cb=cb: e.memset(YACC[:, cb, :], 0.0), writes=[bYA[cb]])
                for (t0, n, r) in TILES:
                    K.dma("sp", XU[:, :n], Us[cb * 128:(cb + 1) * 128, t0:t0 + n], bXU, reads=[B_scr["Us"]], writes=[bXU])
                    K.op("act", lambda e, cb=cb, t0=t0, n=n: e.copy(out=UT[:, cb, t0:t0 + n], in_=XU[:, :n]),
                         reads=[bXU], writes=[bUT])

            def S(nm):
                return sm[nm][0]

            def bS_(nm):
                return sm[nm][1]

            def vop(fn, reads, writes):
                K.op("dve", fn, reads=[bS_(x) if isinstance(x, str) else x for x in reads],
                     writes=[bS_(x) if isinstance(x, str) else x for x in writes])

            MAGIC = 12582912.0
            INV2PI = 1.0 / TWO_PI

            def sincos(angle_t, bang, out_sin, bsin, out_cos, bcos, shape_cols):
                tmp, btmp = (ANG2, bANG2) if shape_cols == 256 else sm["AE2"]
                uu, buu = (ANG3, bANG3) if shape_cols == 256 else sm["AE3"]
                c = shape_cols
                for (dst, bdst, off) in ((out_sin, bsin, 0.0), (out_cos, bcos, math.pi / 2)):
                    K.op("dve", lambda e, off=off: e.tensor_scalar(out=uu[:, :c], in0=angle_t, scalar1=off, scalar2=None, op0=ALU.add),
                         reads=[bang], writes=[buu])
                    K.op("dve", lambda e: e.tensor_scalar(out=tmp[:, :c], in0=uu[:, :c], scalar1=INV2PI, scalar2=MAGIC,
                                                          op0=ALU.mult, op1=ALU.add), reads=[buu], writes=[btmp])
                    K.op("dve", lambda e: e.tensor_scalar(out=tmp[:, :c], in0=tmp[:, :c], scalar1=-MAGIC, scalar2=TWO_PI,
                                                          op0=ALU.add, op1=ALU.mult), reads=[btmp], writes=[btmp])
                    K.op("dve", lambda e: e.tensor_tensor(out=uu[:, :c], in0=uu[:, :c], in1=tmp[:, :c], op=ALU.subtract),
                         reads=[buu, btmp], writes=[buu])
                    K.op("dve", lambda e: e.tensor_scalar(out=uu[:, :c], in0=uu[:, :c], scalar1=-3.1415925, scalar2=3.1415925,
                                                          op0=ALU.max, op1=ALU.min), reads=[buu], writes=[buu])
                    K.op("act", lambda e, dst=dst: e.activation(out=dst, in_=uu[:, :c], func=AF.Sin),
                         reads=[buu], writes=[bdst])

            def setup_pset(d):
                ar = SA[:, (d * 3 + 0) * 16:(d * 3 + 0) * 16 + 16]
                ai = SA[:, (d * 3 + 1) * 16:(d * 3 + 1) * 16 + 16]
                ldt = SA[:, (d * 3 + 2) * 16:(d * 3 + 2) * 16 + 16]
                K.op("act", lambda e: e.activation(out=S("DT")[:], in_=ldt, func=AF.Exp), reads=[bSA], writes=[bS_("DT")])
                vop(lambda e: e.tensor_tensor(out=S("LR")[:], in0=ar, in1=S("DT")[:], op=ALU.mult), [bSA, "DT"], ["LR"])
                vop(lambda e: e.tensor_tensor(out=S("LI")[:], in0=ai, in1=S("DT")[:], op=ALU.mult), [bSA, "DT"], ["LI"])
                K.op("act", lambda e: e.activation(out=S("RHO")[:], in_=S("LR")[:], func=AF.Exp), reads=[bS_("LR")], writes=[bS_("RHO")])
                sincos(S("LI")[:], bS_("LI"), S("T1")[:], bS_("T1"), S("T2")[:], bS_("T2"), 16)
                vop(lambda e: e.tensor_tensor(out=S("ABR")[:], in0=S("RHO")[:], in1=S("T2")[:], op=ALU.mult), ["RHO", "T2"], ["ABR"])
                vop(lambda e: e.tensor_tensor(out=S("ABI")[:], in0=S("RHO")[:], in1=S("T1")[:], op=ALU.mult), ["RHO", "T1"], ["ABI"])
                vop(lambda e: e.tensor_scalar(out=S("T3")[:], in0=S("ABR")[:], scalar1=-1.0, scalar2=None, op0=ALU.add), ["ABR"], ["T3"])
                vop(lambda e: e.tensor_tensor(out=S("TL1")[:], in0=ar, in1=ar, op=ALU.mult), [bSA], ["TL1"])
                vop(lambda e: e.tensor_tensor(out=S("TL2")[:], in0=ai, in1=ai, op=ALU.mult), [bSA], ["TL2"])
                vop(lambda e: e.tensor_tensor(out=S("DEN")[:], in0=S("TL1")[:], in1=S("TL2")[:], op=ALU.add), ["TL1", "TL2"], ["DEN"])
                vop(lambda e: e.reciprocal(out=S("DEN")[:], in_=S("DEN")[:]), ["DEN"], ["DEN"])
                vop(lambda e: e.tensor_tensor(out=S("TL1")[:], in0=S("T3")[:], in1=ar, op=ALU.mult), ["T3", bSA], ["TL1"])
                vop(lambda e: e.tensor_tensor(out=S("TL2")[:], in0=S("ABI")[:], in1=ai, op=ALU.mult), ["ABI", bSA], ["TL2"])
                vop(lambda e: e.tensor_tensor(out=S("KR_")[:], in0=S("TL1")[:], in1=S("TL2")[:], op=ALU.add), ["TL1", "TL2"], ["KR_"])
                vop(lambda e: e.tensor_tensor(out=S("KR_")[:], in0=S("KR_")[:], in1=S("DEN")[:], op=ALU.mult), ["KR_", "DEN"], ["KR_"])
                vop(lambda e: e.tensor_tensor(out=S("TL1")[:], in0=S("ABI")[:], in1=ar, op=ALU.mult), ["ABI", bSA], ["TL1"])
                vop(lambda e: e.tensor_tensor(out=S("TL2")[:], in0=S("T3")[:], in1=ai, op=ALU.mult), ["T3", bSA], ["TL2"])
                vop(lambda e: e.tensor_tensor(out=S("KI_")[:], in0=S("TL1")[:], in1=S("TL2")[:], op=ALU.subtract), ["TL1", "TL2"], ["KI_"])
                vop(lambda e: e.tensor_tensor(out=S("KI_")[:], in0=S("KI_")[:], in1=S("DEN")[:], op=ALU.mult), ["KI_", "DEN"], ["KI_"])
                vop(lambda e: e.tensor_scalar(out=S("NKR")[:], in0=S("KR_")[:], scalar1=-1.0, scalar2=None, op0=ALU.mult), ["KR_"], ["NKR"])
                vop(lambda e: e.tensor_scalar(out=S("NKI")[:], in0=S("KI_")[:], scalar1=-1.0, scalar2=None, op0=ALU.mult), ["KI_"], ["NKI"])
                for (T_, er, ei, nei) in ((256.0, "ER", "EI", "NEI"), (128.0, "ER2", "EI2", "NEI2")):
                    vop(lambda e, T_=T_: e.tensor_scalar(out=S("AE")[:], in0=S("LI")[:], scalar1=T_, scalar2=None, op0=ALU.mult), ["LI"], ["AE"])
                    sincos(S("AE")[:], bS_("AE"), S(ei)[:], bS_(ei), S(er)[:], bS_(er), 16)
                    vop(lambda e, ei=ei, nei=nei: e.tensor_scalar(out=S(nei)[:], in0=S(ei)[:], scalar1=-1.0, scalar2=None, op0=ALU.mult), [ei], [nei])
                K.dma("sp", CWf[:], cw_in[li][:, d * 4096:(d + 1) * 4096], bCWf, writes=[bCWf])
                for rb in range(16):
                    K.op("dve", lambda e, rb=rb: e.tensor_scalar(out=ANG[:], in0=IOT[:], scalar1=S("LI")[:, rb:rb + 1], scalar2=None,
                                                                 op0=ALU.mult), reads=[bIOT, bS_("LI")], writes=[bANG])
                    sincos(ANG[:], bANG, TABS[:, rb, :], bTABS, TABC[:, rb, :], bTABC, 256)
                    K.op("dve", lambda e, rb=rb: e.tensor_scalar(out=RHOT[:, rb, :], in0=IOT[:], scalar1=0.0,
                                                                 scalar2=S("RHO")[:, rb:rb + 1], op0=ALU.mult, op1=ALU.add),
                         reads=[bIOT, bS_("RHO")], writes=[bRHOT])
                    cre = CWf[:, rb * 128:(rb + 1) * 128]
                    cim = CWf[:, 2048 + rb * 128:2048 + (rb + 1) * 128]
                    t1_, bt1_ = mt["m1"]
                    t2_, bt2_ = mt["m2"]
                    K.op("dve", lambda e, rb=rb, cre=cre: e.tensor_scalar(out=t1_[:, :128], in0=cre, scalar1=S("KR_")[:, rb:rb + 1],
                                                                          scalar2=None, op0=ALU.mult),
                         reads=[bCWf, bS_("KR_")], writes=[bt1_])
                    K.op("dve", lambda e, rb=rb, cim=cim: e.scalar_tensor_tensor(
                        out=CWb[:, rb * 128:(rb + 1) * 128], in0=cim, scalar=S("NKI")[:, rb:rb + 1], in1=t1_[:, :128],
                        op0=ALU.mult, op1=ALU.add), reads=[bCWf, bS_("NKI"), bt1_], writes=[bCWb])
                    K.op("dve", lambda e, rb=rb, cre=cre: e.tensor_scalar(out=t2_[:, :128], in0=cre, scalar1=S("NKI")[:, rb:rb + 1],
                                                                          scalar2=None, op0=ALU.mult),
                         reads=[bCWf, bS_("NKI")], writes=[bt2_])
                    K.op("dve", lambda e, rb=rb, cim=cim: e.scalar_tensor_tensor(
                        out=CWb[:, 2048 + rb * 128:2048 + (rb + 1) * 128], in0=cim, scalar=S("NKR")[:, rb:rb + 1], in1=t2_[:, :128],
                        op0=ALU.mult, op1=ALU.add), reads=[bCWf, bS_("NKR"), bt2_], writes=[bCWb])

            hctr = [0]

            def scan_pass(d, chunks, rev, c0_scale_col):
                for (c0, T) in chunks:
                    er, ei, nei = ("ER", "EI", "NEI") if T == 256 else ("ER2", "EI2", "NEI2")
                    for cb in range(4):
                        py, bpy = getps(0, 2)
                        for rbl in range(4):
                            rb = cb * 4 + rbl
                            pa, bpa = getps(2, 8)
                            pb, bpb = getps(2, 8)
                            bwr = BW[:, ((d * 2 + 0) * 16 + rb) * 128:((d * 2 + 0) * 16 + rb + 1) * 128]
                            bwi = BW[:, ((d * 2 + 1) * 16 + rb) * 128:((d * 2 + 1) * 16 + rb + 1) * 128]
                            K.op("pe", lambda e, pa=pa, bwr=bwr: e.matmul(pa[:, :T], lhsT=bwr, rhs=UT[:, cb, c0:c0 + T],
                                                                          start=True, stop=True), reads=[bBW, bUT], writes=[bpa])
                            K.op("pe", lambda e, pb=pb, bwi=bwi: e.matmul(pb[:, :T], lhsT=bwi, rhs=UT[:, cb, c0:c0 + T],
                                                                          start=True, stop=True), reads=[bBW, bUT], writes=[bpb])
                            X = pa[:, T - 1::-1] if rev else pa[:, :T]
                            Y = pb[:, T - 1::-1] if rev else pb[:, :T]
                            tc_ = TABC[:, rb, :T]
                            ts_ = TABS[:, rb, :T]

                            def tt(eng, o, a, b_, op, rd, wr):
                                K.op(eng, lambda e: e.tensor_tensor(out=o, in0=a, in1=b_, op=op), reads=rd, writes=wr)

                            def M(nm):
                                return mt[nm][0][:, :T]

                            def bM(nm):
                                return mt[nm][1]

                            tt("dve", M("m1"), X, tc_, ALU.mult, [bpa, bTABC], [bM("m1")])
                            tt("dve", M("m2"), Y, ts_, ALU.mult, [bpb, bTABS], [bM("m2")])
                            tt("pool", M("gir"), M("m1"), M("m2"), ALU.add, [bM("m1"), bM("m2")], [bM("gir")])
                            tt("dve", M("m3"), Y, tc_, ALU.mult, [bpb, bTABC], [bM("m3")])
                            tt("dve", M("m4"), X, ts_, ALU.mult, [bpa, bTABS], [bM("m4")])
                            tt("pool", M("gii"), M("m3"), M("m4"), ALU.subtract, [bM("m3"), bM("m4")], [bM("gii")])
                            K.op("dve", lambda e, rb=rb: e.tensor_tensor_scan(out=M("GR"), data0=RHOT[:, rb, :T], data1=M("gir"),
                                                                              initial=S("CAR")[:, rb:rb + 1], op0=ALU.mult, op1=ALU.add),
                                 reads=[bRHOT, bM("gir"), bS_("CAR")], writes=[bM("GR")])
                            K.op("dve", lambda e, rb=rb: e.tensor_tensor_scan(out=M("GI"), data0=RHOT[:, rb, :T], data1=M("gii"),
                                                                              initial=S("CAI")[:, rb:rb + 1], op0=ALU.mult, op1=ALU.add),
                                 reads=[bRHOT, bM("gii"), bS_("CAI")], writes=[bM("GI")])
                            GRl = mt["GR"][0][:, T - 1:T]
                            GIl = mt["GI"][0][:, T - 1:T]
                            K.op("dve", lambda e, rb=rb: e.tensor_scalar(out=S("T1")[:, rb:rb + 1], in0=GRl, scalar1=S(er)[:, rb:rb + 1],
                                                                         scalar2=None, op0=ALU.mult),
                                 reads=[bM("GR"), bS_(er)], writes=[bS_("T1")])
                            K.op("dve", lambda e, rb=rb: e.scalar_tensor_tensor(out=S("CAR")[:, rb:rb + 1], in0=GIl, scalar=S(nei)[:, rb:rb + 1],
                                                                                in1=S("T1")[:, rb:rb + 1], op0=ALU.mult, op1=ALU.add),
                                 reads=[bM("GI"), bS_(nei), bS_("T1")], writes=[bS_("CAR")])
                            K.op("dve", lambda e, rb=rb: e.tensor_scalar(out=S("T2")[:, rb:rb + 1], in0=GIl, scalar1=S(er)[:, rb:rb + 1],
                                                                         scalar2=None, op0=ALU.mult),
                                 reads=[bM("GI"), bS_(er)], writes=[bS_("T2")])
                            K.op("dve", lambda e, rb=rb: e.scalar_tensor_tensor(out=S("CAI")[:, rb:rb + 1], in0=GRl, scalar=S(ei)[:, rb:rb + 1],
                                                                                in1=S("T2")[:, rb:rb + 1], op0=ALU.mult, op1=ALU.add),
                                 reads=[bM("GR"), bS_(ei), bS_("T2")], writes=[bS_("CAI")])
                            hr, bhr = HRt[hctr[0] % 2]
                            hi, bhi = HIt[hctr[0] % 2]
                            hctr[0] += 1
                            HRo = hr[:, T - 1::-1] if rev else hr[:, :T]
                            HIo = hi[:, T - 1::-1] if rev else hi[:, :T]
                            tt("pool", M("m5"), M("GR"), tc_, ALU.mult, [bM("GR"), bTABC], [bM("m5")])
                            tt("pool", M("m6"), M("GI"), ts_, ALU.mult, [bM("GI"), bTABS], [bM("m6")])
                            tt("dve", HRo, M("m5"), M("m6"), ALU.subtract, [bM("m5"), bM("m6")], [bhr])
                            tt("pool", M("m7"), M("GI"), tc_, ALU.mult, [bM("GI"), bTABC], [bM("m7")])
                            tt("pool", M("m8"), M("GR"), ts_, ALU.mult, [bM("GR"), bTABS], [bM("m8")])
                            tt("dve", HIo, M("m7"), M("m8"), ALU.add, [bM("m7"), bM("m8")], [bhi])
                            K.op("pe", lambda e, rb=rb, hr=hr: e.matmul(py[:, :T], lhsT=CWb[:, rb * 128:(rb + 1) * 128], rhs=hr[:, :T],
                                                                        start=(rbl == 0), stop=False), reads=[bCWb, bhr], writes=[bpy])
                            K.op("pe", lambda e, rb=rb, hi=hi: e.matmul(py[:, :T], lhsT=CWb[:, 2048 + rb * 128:2048 + (rb + 1) * 128],
                                                                        rhs=hi[:, :T], start=False, stop=(rbl == 3)),
                                 reads=[bCWb, bhi], writes=[bpy])
                        ya = YACC[:, cb, c0:c0 + T]
                        if c0 == 0 and c0_scale_col is not None:
                            K.op("dve", lambda e: e.scalar_tensor_tensor(out=ya, in0=py[:, :T], scalar=SEL[:, c0_scale_col:c0_scale_col + 1],
                                                                         in1=ya, op0=ALU.mult, op1=ALU.add),
                                 reads=[bpy, bYA[cb], bSEL], writes=[bYA[cb]])
                        else:
                            K.op("dve", lambda e: e.tensor_tensor(out=ya, in0=py[:, :T], in1=ya, op=ALU.add),
                                 reads=[bpy, bYA[cb]], writes=[bYA[cb]])

            def exchange(w0col, w1col):
                K.op("dve", lambda e: e.tensor_copy(out=XS[:, 0:16], in_=S("CAR")[:]), reads=[bS_("CAR")], writes=[bXS])
                K.op("dve", lambda e: e.tensor_copy(out=XS[:, 16:32], in_=S("CAI")[:]), reads=[bS_("CAI")], writes=[bXS])
                K.dma("sp", XIN[:, :], XS[:], bXS, reads=[bXS], writes=[B_scr["XIN"]])
                K.coll([XIN], [XOUT], [B_scr["XIN"]], [B_scr["XOUT"]])
                K.dma("sp", XG0[:], XOUT[0:128, :], bXG0, reads=[B_scr["XOUT"]], writes=[bXG0])
                K.dma("sp", XG1[:], XOUT[128:256, :], bXG1, reads=[B_scr["XOUT"]], writes=[bXG1])
                for (nm, lo) in (("CAR", 0), ("CAI", 16)):
                    if w0col is None:
                        vop(lambda e: e.memset(S(nm)[:], 0.0), [], [nm])
                    else:
                        vop(lambda e: e.tensor_scalar(out=S(nm)[:], in0=XG0[:, lo:lo + 16], scalar1=SEL[:, w0col:w0col + 1], scalar2=None,
                                                      op0=ALU.mult), [bXG0, bSEL], [nm])
                    if w1col is not None:
                        vop(lambda e: e.scalar_tensor_tensor(out=S(nm)[:], in0=XG1[:, lo:lo + 16], scalar=SEL[:, w1col:w1col + 1],
                                                             in1=S(nm)[:], op0=ALU.mult, op1=ALU.add), [bXG1, bSEL, nm], [nm])

            fwd_chunks = [(0, 256), (256, 256), (512, 256), (768, 256), (1024, 128)]
            setup_pset(1)
            vop(lambda e: e.memset(S("CAR")[:], 0.0), [], ["CAR"])
            vop(lambda e: e.memset(S("CAI")[:], 0.0), [], ["CAI"])
            scan_pass(1, [(0, 256)], True, 2)
            exchange(4, None)
            setup_pset(0)
            scan_pass(0, fwd_chunks, False, None)
            exchange(6, 7)
            setup_pset(1)
            scan_pass(1, fwd_chunks[::-1], True, 3)
            K.barrier()
            ph2.close()
            ph = _ph_outer
            XU, bXU = sb("XU2", [128, 512], F32, ph)
            TA, bTA = sb("TA5", [128, 256], F32, ph)
            SD, bSD = sb("SD", [128, 4], F32, ph)
            BG, bBG = sb("BG", [128, 4], F32, ph)
            MN, bMN = sb("MN5", [128, 16], F32, ph)
            K.dma("sp", SD[:], ssd[li], bSD, writes=[bSD])
            K.dma("sp", BG[:], bglu[li], bBG, writes=[bBG])
            K.dma("sp", MN[:], mng[li], bMN, writes=[bMN])
            GB, bGB = UT, bUT
            WG = [sb("WG%d" % i, [128, 4, 128], BF16, ph) for i in range(6)]
            for cb in range(4):
                for (t0, n, r) in TILES:
                    for q in range(0, n, 256):
                        c0 = t0 + q
                        w = min(256, n - q)
                        ya = YACC[:, cb, c0:c0 + w]
                        ta = TA[:, :w]
                        K.dma("sp", XU[:, :w], Us[cb * 128:(cb + 1) * 128, c0:c0 + w], bXU, reads=[B_scr["Us"]], writes=[bXU])
                        K.op("dve", lambda e, ya=ya, cb=cb: e.scalar_tensor_tensor(out=ya, in0=XU[:, :w], scalar=SD[:, cb:cb + 1], in1=ya,
                                                                                   op0=ALU.mult, op1=ALU.add),
                             reads=[bXU, bSD, bYA[cb]], writes=[bYA[cb]])
                        K.op("dve", lambda e, ya=ya: e.tensor_tensor(out=ta, in0=ya, in1=ya, op=ALU.mult), reads=[bYA[cb]], writes=[bTA])
                        K.op("dve", lambda e: e.tensor_scalar(out=ta, in0=ta, scalar1=0.044715, scalar2=1.0, op0=ALU.mult, op1=ALU.add),
                             reads=[bTA], writes=[bTA])
                        K.op("dve", lambda e, ya=ya: e.tensor_tensor(out=ta, in0=ta, in1=ya, op=ALU.mult), reads=[bTA, bYA[cb]], writes=[bTA])
                        K.op("act", lambda e: e.activation(out=ta, in_=ta, func=AF.Sigmoid, scale=1.5957691216057308),
                             reads=[bTA], writes=[bTA])
                        K.op("dve", lambda e, ya=ya: e.tensor_tensor(out=ya, in0=ya, in1=ta, op=ALU.mult), reads=[bTA, bYA[cb]], writes=[bYA[cb]])
                        K.op("act", lambda e, ya=ya, cb=cb, c0=c0: e.copy(out=GB[:, cb, c0:c0 + w], in_=ya), reads=[bYA[cb]], writes=[bGB])
            wgv = wview(w_glu[li], 4)
            groups = [[(wgv, j * 128, 128)] for j in range(4)]
            YS, bYS_ = sb("YS", [128, 4, NL], F32, ph)
            bYSj = [Buf("YS%d" % j) for j in range(4)]
            SGt = [sb("SGt%d" % i, [128, 512], F32, ph) for i in range(2)]

            def cons_g(gi, tile, outs):
                (t0, n, r) = tile
                sg, bsg = SGt[gi % 2]
                ps, bps, M = outs[0]
                K.op("act", lambda e: e.activation(out=sg[:, :n], in_=ps[:, :n], func=AF.Sigmoid, bias=BG[:, gi:gi + 1]),
                     reads=[bps, bBG], writes=[bsg])
                K.op("dve", lambda e: e.tensor_tensor(out=YS[:, gi, t0:t0 + n], in0=YACC[:, gi, t0:t0 + n], in1=sg[:, :n], op=ALU.mult),
                     reads=[bsg, bYA[gi]], writes=[bYSj[gi]])

            proj(groups, 4, lambda kb, t0, n: GB[:, kb, t0:t0 + n], [bGB], TILES, cons_g, WG)
            STB = [sb("STB5%d" % i, [128, 512], BF16, ph) for i in range(3)]
            merge_norm(lambda t0, n: [(YS[:, j, t0:t0 + n], bYSj[j]) for j in range(4)], 512, 12, 1536, TILES, ph, "s", MN, bMN, STB)
            K.barrier()

        with ExitStack() as ph:
            AX, bAX_ = sb("AX", [128, 8, 512], F32, ph)
            bAX = [Buf("AX%d" % j) for j in range(8)]
            MN, bMN = sb("MNa", [128, 16], F32, ph)
            K.dma("sp", MN[:], mng[li], bMN, writes=[bMN])
            STB = [sb("STBa%d" % i, [128, 512], BF16, ph) for i in range(3)]

            def att_blocks(t0, n):
                for j in range(8):
                    K.dma("sp", AX[:, j, :n], ATT[j * 128:(j + 1) * 128, t0:t0 + n], bAX[j], reads=[B_scr["ATT"]], writes=[bAX[j]])
                return [(AX[:, j, :n], bAX[j]) for j in range(8)]

            merge_norm(att_blocks, 1024, 4, 512, TILES, ph, "a", MN, bMN, STB)
            K.barrier()

        with ExitStack() as ph:
            MT, bMT_ = sb("MT", [128, 16, NL], BF16, ph)
            bMTk = [Buf("MT%d" % k) for k in range(16)]
            for kb in range(16):
                K.dma("sp", MT[:, kb, :], MIX[kb * 128:(kb + 1) * 128, :], bMTk[kb], reads=[B_scr["MIX"]], writes=[bMTk[kb]])
            WP = [sb("WO%d" % i, [128, 16, 128], BF16, ph) for i in range(6)]
            XI = [sb("XI%d" % i, [128, 512], F32, ph) for i in range(3)]
            wov = wview(w_o[li], 16)
            groups = [[(wov, j * 128, 128)] for j in range(16)]
            xctr = [0]

            def cons_o(gi, tile, outs):
                (t0, n, r) = tile
                xi, bxi = XI[xctr[0] % 3]
                xctr[0] += 1
                ps, bps, M = outs[0]
                K.dma("sp", xi[:, :n], XR[gi * 128:(gi + 1) * 128, t0:t0 + n], bxi, reads=[B_XR[gi]], writes=[bxi])
                K.op("dve", lambda e: e.scalar_tensor_tensor(out=xi[:, :n], in0=ps[:, :n], scalar=GG1[:, r * 16 + gi:r * 16 + gi + 1],
                                                             in1=xi[:, :n], op0=ALU.mult, op1=ALU.add),
                     reads=[bps, bGG1, bxi], writes=[bxi])
                K.dma("sp", XR[gi * 128:(gi + 1) * 128, t0:t0 + n], xi[:, :n], bxi, reads=[bxi], writes=[B_XR[gi]])

            proj(groups, 16, lambda kb, t0, n: MT[:, kb, t0:t0 + n], bMTk, TILES, cons_o, WP)
            K.barrier()

        with ExitStack() as ph:
            H2, bH2 = sb("H2", [128, 16, NL], BF16, ph)
            with ExitStack() as ph2:
                norm_mod(TILES, H2, bH2, GS2, bGS2, SH2, bSH2, ph2)
                K.barrier()
            HID, bHID_ = sb("HID", [128, 44, NL], BF16, ph)
            bHIDj = [Buf("HID%d" % j) for j in range(44)]
            WP = [sb("WF%d" % i, [128, 16, 128], BF16, ph) for i in range(6)]
            S1 = [sb("S1f%d" % i, [128, 512], F32, ph) for i in range(2)]
            wgv_ = wview(w_gate[li], 16)
            wuv_ = wview(w_up[li], 16)
            groups = [[(wgv_, j * 128, 128), (wuv_, j * 128, 128)] for j in range(44)]

            def cons_f1(gi, tile, outs):
                (t0, n, r) = tile
                s1, bs1 = S1[gi % 2]
                K.op("act", lambda e: e.activation(out=s1[:, :n], in_=outs[0][0][:, :n], func=AF.Silu),
                     reads=[outs[0][1]], writes=[bs1])
                K.op("dve", lambda e: e.tensor_tensor(out=HID[:, gi, t0:t0 + n], in0=outs[1][0][:, :n], in1=s1[:, :n],
                                                      op=ALU.mult), reads=[outs[1][1], bs1], writes=[bHIDj[gi]])

            proj(groups, 16, lambda kb, t0, n: H2[:, kb, t0:t0 + n], [bH2], TILES, cons_f1, WP)
            WD = [sb("WD%d" % i, [128, 44, 128], BF16, ph) for i in range(2)]
            XI = [sb("XIf%d" % i, [128, 512], F32, ph) for i in range(3)]
            wdv = wview(w_down[li], 44)
            xctr = [0]
            for nb_ in range(16):
                W, bW = WD[nb_ % 2]
                K.dma("pool", W[:], wdv[:, :, nb_ * 128:(nb_ + 1) * 128], bW, writes=[bW])
                for (t0, n, r) in TILES:
                    ps, bps = getps()
                    for kb in range(44):
                        K.op("pe", lambda e, kb=kb, W=W, ps=ps: e.matmul(ps[:, :n], lhsT=W[:, kb, :], rhs=HID[:, kb, t0:t0 + n],
                                                                         start=(kb == 0), stop=(kb == 43)),
                             reads=[bW, bHIDj[kb]], writes=[bps])
                    xi, bxi = XI[xctr[0] % 3]
                    xctr[0] += 1
                    K.dma("sp", xi[:, :n], XR[nb_ * 128:(nb_ + 1) * 128, t0:t0 + n], bxi, reads=[B_XR[nb_]], writes=[bxi])
                    K.op("dve", lambda e, xi=xi, ps=ps: e.scalar_tensor_tensor(
                        out=xi[:, :n], in0=ps[:, :n], scalar=GG2[:, r * 16 + nb_:r * 16 + nb_ + 1], in1=xi[:, :n],
                        op0=ALU.mult, op1=ALU.add), reads=[bps, bGG2, bxi], writes=[bxi])
                    K.dma("sp", XR[nb_ * 128:(nb_ + 1) * 128, t0:t0 + n], xi[:, :n], bxi, reads=[bxi], writes=[B_XR[nb_]])
            K.barrier()

    with ExitStack() as ph:
        XT, bXT = sb("XTf", [128, 16, 512], F32, ph)
        bXTk = [Buf("XTf%d" % k) for k in range(16)]
        SQ, bSQ, RS, bRS, TM = norm_scratch(ph, "f")
        OT = [sb("OTf%d" % i, [128, 512], F32, ph) for i in range(2)]
        FG, bFG = sb("FG", [128, 16], F32, ph)
        K.dma("sp", FG[:], fng[:, :], bFG, writes=[bFG])
        for (c0, n, r) in TILES:
            for kb in range(16):
                K.dma("sp", XT[:, kb, :n], XR[kb * 128:(kb + 1) * 128, c0:c0 + n], bXTk[kb], reads=[B_XR[kb]], writes=[bXTk[kb]])
            ps, bps = sumsq([(XT[:, kb, :n], bXTk[kb]) for kb in range(16)], n, SQ, bSQ)
            rstd_from_psum(ps, bps, n, D, RS, bRS)
            for kb in range(16):
                tm, btm = TM[kb % 2]
                ot, bot = OT[kb % 2]
                K.op("dve", lambda e, kb=kb, tm=tm: e.tensor_tensor(out=tm[:, :n], in0=XT[:, kb, :n], in1=RS[:, :n], op=ALU.mult),
                     reads=[bXTk[kb], bRS], writes=[btm])
                K.op("act", lambda e, kb=kb, tm=tm, ot=ot: e.activation(out=ot[:, :n], in_=tm[:, :n], func=AF.Identity, scale=FG[:, kb:kb + 1]),
                     reads=[btm, bFG], writes=[bot])
                K.dma("sp", outT[kb * 128:(kb + 1) * 128, c0:c0 + n], ot[:, :n], bot, reads=[bot], writes=[Buf()])
        if dbg:
            T0, bT0 = sb("T0d", [128, NT], F32, ph)
            T0b, bT0b = sb("T0bd", [128, NT], BF16, ph)
            for nm, ap_, shp, dt_ in (("XR", XR, [D, NL], F32), ("MIX", MIX, [D, NL], BF16), ("ATT", ATT, [1024, NL], F32),
                                      ("Us", Us, [512, NL], F32), ("QN", QN, [1024, NL], BF16), ("QR", QR, [512, NL], BF16),
                                      ("KRR", KRR, [64, NT], BF16), ("CQ", CQ, [512, NL], F32), ("KN", KN, [1024, NT], BF16),
                                      ("GOUT", GOUT, [2 * GROWS, NL], F32)):
                o = nc.dram_tensor("dbg_" + nm, shp, dt_, kind="ExternalOutput").ap()
                Td, bTd = (T0, bT0) if dt_ == F32 else (T0b, bT0b)
                for r0 in range(0, shp[0], 128):
                    rr = min(128, shp[0] - r0)
                    K.dma("sp", Td[:rr, :shp[1]], ap_[r0:r0 + rr, :], bTd, writes=[bTd])
                    K.dma("sp", o[r0:r0 + rr, :], Td[:rr, :shp[1]], bTd, reads=[bTd], writes=[Buf()])
        K.barrier()
    es.close()
    return nc


def _pl(v, nb):
    return np.ascontiguousarray(v.reshape(nb, 128).T)


def _rope_perm():
    perm = np.zeros(64, np.int64)
    for r in range(64):
        axis, half, f = r // 32, (r % 32) // 16, r % 16
        perm[r] = axis * 32 + (1 - half) * 16 + f
    return perm


def _rope_tables(pos):
    f32 = np.float32
    n = len(pos)
    p = np.maximum(pos, 0)
    row = (p // 64).astype(f32)
    col = (p % 64).astype(f32)
    inv = (np.float32(10000.0) ** (-np.arange(16, dtype=f32) / np.float32(16))).astype(f32)
    rc = np.zeros((64, n), f32)
    rs = np.zeros((64, n), f32)
    for r in range(64):
        axis, half, f = r // 32, (r % 32) // 16, r % 16
        ang = ((row if axis == 0 else col) * inv[f]).astype(f32)
        c = np.cos(ang)
        s = np.sin(ang) * (-1.0 if half == 0 else 1.0)
        c[pos < 0] = 1.0
        s[pos < 0] = 0.0
        rc[r] = c
        rs[r] = s
    return rc, rs


POS_A = np.concatenate([-np.ones(CTX, np.int64), np.arange(0, NL - CTX)])
POS_B = (LAT - 1) - np.arange(NL)


def _prep_shared(inp, nl):
    f32 = np.float32
    sh = {}
    sh["w_ada"] = np.ascontiguousarray(inp["w_ada"][:nl])
    sh["bada2"] = np.ascontiguousarray(np.stack([inp["b_ada"][:nl], inp["b_ada"][:nl]], axis=1))
    sh["n1g"] = np.stack([_pl(inp["norm1_g"][i], 16) for i in range(nl)])
    sh["n2g"] = np.stack([_pl(inp["norm2_g"][i], 16) for i in range(nl)])
    sh["mng"] = np.stack([_pl(inp["mix_norm"][i], 16) for i in range(nl)])
    sh["qng"] = np.stack([_pl(inp["mla_q_norm"][i], 4) for i in range(nl)])
    sh["kvng"] = np.stack([_pl(inp["mla_kv_norm"][i], 2) for i in range(nl)])
    sh["ssd"] = np.stack([_pl(inp["ssm_d"][i], 4) for i in range(nl)])
    sh["bglu"] = np.stack([_pl(inp["b_glu"][i], 4) for i in range(nl)])
    sh["fng"] = _pl(inp["final_norm"], 16)
    sh["w_in"] = np.ascontiguousarray(inp["w_in"][:nl])
    perm = _rope_perm()
    sh["w_krp"] = np.ascontiguousarray(inp["w_in"][:nl][:, :, 2304 + perm])
    sh["w_uq"] = np.ascontiguousarray(inp["w_uq"][:nl])
    cols = np.concatenate([192 * h + 128 + perm for h in range(8)])
    sh["w_uqp"] = np.ascontiguousarray(inp["w_uq"][:nl][:, :, cols])
    sh["w_ukv"] = np.ascontiguousarray(inp["w_ukv"][:nl])
    vcols = np.concatenate([256 * h + 128 + np.arange(128) for h in range(8)])
    sh["w_v"] = np.ascontiguousarray(inp["w_ukv"][:nl][:, :, vcols])
    kc, ks = _rope_tables(np.concatenate([POS_A, POS_B]))
    sh["ropkc"] = kc
    sh["ropks"] = ks
    sh["w_glu"] = np.ascontiguousarray(inp["w_glu"][:nl])
    sh["w_o"] = np.ascontiguousarray(inp["w_o"][:nl])
    sh["w_gate"] = np.ascontiguousarray(inp["w_gate"][:nl])
    sh["w_up"] = np.ascontiguousarray(inp["w_up"][:nl])
    sh["w_down"] = np.ascontiguousarray(inp["w_down"][:nl])
    sh["iota"] = np.ascontiguousarray(np.broadcast_to(np.arange(256, dtype=f32), (128, 256)))
    sh["i2"] = np.eye(2, dtype=f32)
    return sh


def _prep_role(inp, nl, half):
    f32 = np.float32
    ro = {}
    cw = inp["conv_w"][:nl]
    if half == 1:
        cw = cw[:, ::-1, :]
    ro["convw"] = np.ascontiguousarray(cw.reshape(nl, 3, 4, 128).transpose(0, 3, 2, 1).reshape(nl, 128, 12))
    qc, qs = _rope_tables(POS_A if half == 0 else POS_B)
    ro["ropqc"] = qc
    ro["ropqs"] = qs
    ml = np.ones(NL, f32)
    mr = np.ones(NL, f32)
    ml[0] = 0.0
    if half == 0:
        ml[CTX] = 0.0
        mr[CTX - 1] = 0.0
    ro["maskl"] = np.ascontiguousarray(np.broadcast_to(ml, (128, NL)))
    ro["maskr"] = np.ascontiguousarray(np.broadcast_to(mr, (128, NL)))
    sel = np.zeros(8, f32)
    if half == 0:
        sel[[1, 2, 7]] = 1.0
    else:
        sel[[0, 3, 4, 6]] = 1.0
    ro["sel"] = np.ascontiguousarray(np.broadcast_to(sel, (128, 8)))
    km = np.zeros(18, f32)
    if half == 0:
        km[2:] = -30000.0
    ro["kmask"] = np.ascontiguousarray(np.broadcast_to(km, (128, 18)))
    dirs = (0, 1) if half == 0 else (1, 0)
    ssa = np.zeros((nl, 128, 2 * 3 * 16), f32)
    bw = np.zeros((nl, 128, 2, 2, 16, 128), f32)
    cwt = np.zeros((nl, 128, 2, 2, 16, 128), f32)
    for i in range(nl):
        for d, rd in enumerate(dirs):
            ssa[i, :, (d * 3 + 0) * 16:(d * 3 + 0) * 16 + 16] = _pl(inp["ssm_a_re"][i, rd].reshape(-1), 16)
            ssa[i, :, (d * 3 + 1) * 16:(d * 3 + 1) * 16 + 16] = _pl(inp["ssm_a_im"][i, rd].reshape(-1), 16)
            ssa[i, :, (d * 3 + 2) * 16:(d * 3 + 2) * 16 + 16] = _pl(np.repeat(inp["ssm_log_dt"][i, rd], 64), 16)
            for ri, (bsrc, csrc) in enumerate(((inp["ssm_b_re"], inp["ssm_c_re"]), (inp["ssm_b_im"], inp["ssm_c_im"]))):
                for g in range(32):
                    rb, g2 = g // 2, g % 2
                    k0 = (g % 8) * 16
                    bw[i, k0:k0 + 16, d, ri, rb, g2 * 64:(g2 + 1) * 64] = bsrc[i, rd, g].T
                    cwt[i, g2 * 64:(g2 + 1) * 64, d, ri, rb, k0:k0 + 16] = csrc[i, rd, g].T
    ro["ssa"] = ssa
    ro["bw"] = bw.reshape(nl, 128, -1)
    ro["cw"] = cwt.reshape(nl, 128, -1)
    return ro


def _prep_core(inp, b, half):
    if half == 0:
        xloc = np.concatenate([inp["ctx"][b], inp["x"][b][:NL - CTX]], axis=0)
        cc = np.stack([inp["c"][b], inp["c_ctx"]], axis=0)
    else:
        xloc = inp["x"][b][NL - CTX:][::-1]
        cc = np.stack([inp["c"][b], inp["c"][b]], axis=0)
    xT = np.ascontiguousarray(xloc.T)
    cs = np.ascontiguousarray(cc.reshape(2, 16, 128).transpose(2, 1, 0).reshape(128, 32))
    return {"xT": xT, "cs": cs}


def kernel(**inputs):
    inp = {k: np.asarray(v) for k, v in inputs.items()}
    nl = inp["w_ada"].shape[0]
    nc = build(nl)
    shared = _prep_shared(inp, nl)
    roles = [_prep_role(inp, nl, 0), _prep_role(inp, nl, 1)]
    in_maps = []
    for core in range(8):
        m = dict(shared)
        m.update(roles[core % 2])
        m.update(_prep_core(inp, core // 2, core % 2))
        in_maps.append(m)
    res = run_bass_kernel_spmd(nc, in_maps, core_ids=list(range(8)))
    out = np.zeros((4, LAT, D), np.float32)
    for b in range(4):
        oa = np.asarray(res.results[2 * b]["outT"])
        ob = np.asarray(res.results[2 * b + 1]["outT"])
        out[b, :NL - CTX] = oa[:, CTX:].T
        out[b, NL - CTX:] = ob[:, ::-1].T
    return out
```
